# Optimizing a Trainium2 kernel written in Bass

```python
import math
import jax, jax.numpy as jnp
from jax import lax
import numpy as np

D_MODEL = 4096
BATCH = 2
SEQ = 4096
DEPTH = 4

DA_WIDTH = D_MODEL // 2
DA_HEADS = 16
DA_HEAD_DIM = DA_WIDTH // (2 * DA_HEADS)
SC_WIDTH = D_MODEL - DA_WIDTH
CONV_W = 3
EV_SPLITS = (DA_WIDTH, DA_WIDTH, DA_WIDTH, SC_WIDTH, SC_WIDTH, SC_WIDTH)
EV_IN = sum(EV_SPLITS)
RW_WIDTH = D_MODEL // 2
RW_HEAD_DIM = 64
RW_HEADS = RW_WIDTH // RW_HEAD_DIM
W_LORA = 96
A_LORA = 96
V_LORA = 64
G_LORA = 256
RW_SPLITS = (RW_WIDTH, RW_WIDTH, RW_WIDTH, W_LORA, A_LORA, V_LORA, G_LORA)
RW_IN = sum(RW_SPLITS)
LNX_EPS = 64e-5
SA_WIDTH = D_MODEL - RW_WIDTH
SA_HEAD_DIM = 128
SA_HEADS = SA_WIDTH // SA_HEAD_DIM
IDX_HEADS = 16
IDX_DIM = 64
TOPK_MAX = 256
SA_SPLITS = (SA_WIDTH, SA_HEAD_DIM, SA_HEAD_DIM, IDX_HEADS * IDX_DIM, IDX_DIM, IDX_HEADS)
OD_IN = RW_IN + sum(SA_SPLITS)
MIX_OUT = D_MODEL
FFN_HIDDEN = -(-8 * D_MODEL // (3 * 256)) * 256
ROPE_THETA = 10000.0
Q_BLOCK = 128
EPS = 1e-6

kernel_name = 'hybrid_diffattn_shortconv_rwkv7_dsa'


def _split(z, sizes):
    out, off = [], 0
    for s in sizes:
        out.append(z[..., off:off + s])
        off += s
    return out


def _rms(x, g):
    xf = x.astype(jnp.float32)
    y = xf * lax.rsqrt(jnp.mean(xf * xf, axis=-1, keepdims=True) + EPS)
    return (y * g.astype(jnp.float32)).astype(x.dtype)


def _rope(x, pos):
    half = x.shape[-1] // 2
    inv = ROPE_THETA ** (-jnp.arange(half, dtype=jnp.float32) / half)
    ang = pos.astype(jnp.float32)[:, None] * inv[None, :]
    shp = (pos.shape[0],) + (1,) * (x.ndim - 3) + (half,)
    cos, sin = jnp.cos(ang).reshape(shp), jnp.sin(ang).reshape(shp)
    xf = x.astype(jnp.float32)
    x1, x2 = xf[..., :half], xf[..., half:]
    return jnp.concatenate([x1 * cos - x2 * sin, x2 * cos + x1 * sin], axis=-1).astype(x.dtype)


def _diff_attention(q, k, v, lam):
    B, T, H, _, Dh = q.shape
    scale = Dh ** -0.5
    k_pos = jnp.arange(T)

    def block(i):
        start = i * Q_BLOCK
        qb = lax.dynamic_slice_in_dim(q, start, Q_BLOCK, axis=1)
        s = jnp.einsum('bqhcd,bshcd->bhcqs', qb, k, preferred_element_type=jnp.float32) * scale
        q_pos = start + jnp.arange(Q_BLOCK)
        s = jnp.where(k_pos[None, :] <= q_pos[:, None], s, -jnp.inf)
        p = jax.nn.softmax(s, axis=-1)
        a = p[:, :, 0] - lam * p[:, :, 1]
        return jnp.einsum('bhqs,bshd->bqhd', a.astype(v.dtype), v)

    out = lax.map(block, jnp.arange(T // Q_BLOCK))
    return out.transpose(1, 0, 2, 3, 4).reshape(B, T, H, v.shape[-1])


def _dsa_attention(q, k, v, q_idx, k_idx, w_idx, k_sel):
    B, T, H, D = q.shape
    scale = D ** -0.5
    k_pos = jnp.arange(T)
    gather = jax.vmap(lambda a, idx: a[idx])

    def block(i):
        start = i * Q_BLOCK
        qb = lax.dynamic_slice_in_dim(q, start, Q_BLOCK, axis=1)
        qib = lax.dynamic_slice_in_dim(q_idx, start, Q_BLOCK, axis=1)
        wib = lax.dynamic_slice_in_dim(w_idx, start, Q_BLOCK, axis=1)
        q_pos = start + jnp.arange(Q_BLOCK)
        rel = jax.nn.relu(jnp.einsum('bqhd,bsd->bqhs', qib, k_idx, preferred_element_type=jnp.float32))
        score = jnp.einsum('bqhs,bqh->bqs', rel, wib.astype(jnp.float32))
        score = jnp.where((k_pos[None, :] <= q_pos[:, None])[None], score, -jnp.inf)
        _, sel = lax.top_k(score, k_sel)
        valid = sel <= q_pos[None, :, None]
        ks, vs = gather(k, sel), gather(v, sel)
        s = jnp.einsum('bqhd,bqkd->bhqk', qb, ks, preferred_element_type=jnp.float32) * scale
        s = jnp.where(valid[:, None], s, -jnp.inf)
        p = jax.nn.softmax(s, axis=-1)
        return jnp.einsum('bhqk,bqkd->bqhd', p.astype(vs.dtype), vs)

    out = lax.map(block, jnp.arange(T // Q_BLOCK))
    return out.transpose(1, 0, 2, 3, 4).reshape(B, T, H, D)


def _rwkv7_scan(r, decay, k, v, a_vec, b_vec):
    B, T, H, N = r.shape

    def step(S, inp):
        r_t, w_t, k_t, v_t, a_t, b_t = inp
        sa = jnp.einsum('bhvk,bhk->bhv', S, a_t)
        S = S * w_t[:, :, None, :] + sa[..., None] * b_t[:, :, None, :] + v_t[..., None] * k_t[:, :, None, :]
        return S, jnp.einsum('bhvk,bhk->bhv', S, r_t)

    xs = tuple(jnp.moveaxis(t, 1, 0) for t in (r, decay, k, v, a_vec, b_vec))
    _, ys = lax.scan(step, jnp.zeros((B, H, N, N), jnp.float32), xs)
    return jnp.moveaxis(ys, 0, 1)


def _even_mixer(h, w_in, w_out, q_norm, k_norm, lam_p, subln, conv_w, pos, lam_init):
    B, T, _ = h.shape
    q, k, v, gb, gc, u = _split(h @ w_in, EV_SPLITS)
    q = _rope(_rms(q.reshape(B, T, DA_HEADS, 2, DA_HEAD_DIM), q_norm), pos)
    k = _rope(_rms(k.reshape(B, T, DA_HEADS, 2, DA_HEAD_DIM), k_norm), pos)
    lp = lam_p.astype(jnp.float32)
    lam = jnp.exp(jnp.sum(lp[0] * lp[1])) - jnp.exp(jnp.sum(lp[2] * lp[3])) + lam_init
    o = _diff_attention(q, k, v.reshape(B, T, DA_HEADS, 2 * DA_HEAD_DIM), lam)
    o = _rms(o, subln) * (1.0 - lam_init)
    cu = gc * u
    cup = jnp.pad(cu, ((0, 0), (CONV_W - 1, 0), (0, 0)))
    conv = cup[:, 0:T] * conv_w[0]
    for j in range(1, CONV_W):
        conv = conv + cup[:, j:j + T] * conv_w[j]
    y = gb * conv
    mix = jnp.concatenate([o.reshape(B, T, DA_WIDTH), y.astype(o.dtype)], axis=-1) @ w_out
    return mix, v


def _odd_mixer(h, w_in, w_out, mu, w0, w2, a0, a2, v0, v2, g2, k_k, k_a, r_k, lnx_w, lnx_b,
               q_norm, k_norm, idxk_norm, v_first, pos, k_sel):
    B, T, _ = h.shape
    f32 = jnp.float32
    z = h @ w_in
    zr, zd = z[..., :RW_IN], z[..., RW_IN:]
    zr_prev = jnp.pad(zr, ((0, 0), (1, 0), (0, 0)))[:, :-1]
    zr = (zr + (zr_prev - zr) * mu).astype(f32)
    r, k, v, wd, ad, vd, gd = _split(zr, RW_SPLITS)
    logw = -jax.nn.softplus(-(w0 + jnp.tanh(wd) @ w2)) - 0.5
    decay = jnp.exp(-jnp.exp(logw))
    a = jax.nn.sigmoid(a0 + ad @ a2)
    v = v + (v_first.astype(f32) - v) * jax.nn.sigmoid(v0 + vd @ v2)
    g = jax.nn.sigmoid(gd) @ g2
    heads = lambda t: t.reshape(B, T, RW_HEADS, RW_HEAD_DIM)
    per_head = lambda p: p.reshape(RW_HEADS, RW_HEAD_DIM)
    kk = heads(k * k_k)
    kk = kk / jnp.maximum(jnp.sqrt(jnp.sum(kk * kk, axis=-1, keepdims=True)), 1e-12)
    k = k * (1.0 + (a - 1.0) * k_a)
    rh, kh, vh = heads(r), heads(k), heads(v)
    y = _rwkv7_scan(rh, heads(decay), kh, vh, -kk, kk * heads(a))
    mean = jnp.mean(y, axis=-1, keepdims=True)
    var = jnp.mean((y - mean) ** 2, axis=-1, keepdims=True)
    y = (y - mean) * lax.rsqrt(var + LNX_EPS) * per_head(lnx_w) + per_head(lnx_b)
    y = y + jnp.sum(rh * kh * r_k, axis=-1, keepdims=True) * vh
    rw_out = (y.reshape(B, T, RW_WIDTH) * g).astype(h.dtype)
    qd, kd, vdd, qi, ki, wi = _split(zd, SA_SPLITS)
    q = _rope(_rms(qd.reshape(B, T, SA_HEADS, SA_HEAD_DIM), q_norm), pos)
    kd = _rope(_rms(kd, k_norm), pos)
    qi = _rope(qi.reshape(B, T, IDX_HEADS, IDX_DIM), pos)
    ki = _rope(_rms(ki, idxk_norm), pos)
    wi = wi * (IDX_HEADS ** -0.5 * IDX_DIM ** -0.5)
    sa_out = _dsa_attention(q, kd, vdd, qi, ki, wi, k_sel).reshape(B, T, SA_WIDTH)
    return jnp.concatenate([rw_out, sa_out.astype(h.dtype)], axis=-1) @ w_out


def setup_inputs(seed: int = 0) -> dict:
    key = jax.random.key(seed)
    keys = jax.random.split(key, 64)
    it = iter(range(64))
    nk = lambda: keys[next(it)]
    nrm = lambda shape, scale: jax.random.normal(nk(), shape, jnp.float32) * scale
    gain = lambda shape: 1.0 + 0.02 * jax.random.normal(nk(), shape, jnp.float32)
    ne, no = (DEPTH + 1) // 2, DEPTH // 2
    F = FFN_HIDDEN
    return {
        'x': nrm((BATCH, SEQ, D_MODEL), 1.0),
        'mix_norm': gain((DEPTH, D_MODEL)),
        'ffn_norm': gain((DEPTH, D_MODEL)),
        'ffn_gate': nrm((DEPTH, D_MODEL, F), D_MODEL ** -0.5),
        'ffn_up': nrm((DEPTH, D_MODEL, F), D_MODEL ** -0.5),
        'ffn_down': nrm((DEPTH, F, D_MODEL), F ** -0.5),
        'ev_w_in': nrm((ne, D_MODEL, EV_IN), D_MODEL ** -0.5),
        'ev_w_out': nrm((ne, MIX_OUT, D_MODEL), MIX_OUT ** -0.5),
        'da_q_norm': gain((ne, DA_HEAD_DIM)),
        'da_k_norm': gain((ne, DA_HEAD_DIM)),
        'da_lambda': nrm((ne, 4, DA_HEAD_DIM), 0.1),
        'da_subln': gain((ne, 2 * DA_HEAD_DIM)),
        'sc_conv': nrm((ne, CONV_W, SC_WIDTH), CONV_W ** -0.5),
        'od_w_in': nrm((no, D_MODEL, OD_IN), D_MODEL ** -0.5),
        'od_w_out': nrm((no, MIX_OUT, D_MODEL), MIX_OUT ** -0.5),
        'rw_mu': jax.random.uniform(nk(), (no, RW_IN), jnp.float32, 0.0, 1.0),
        'rw_w0': jax.random.uniform(nk(), (no, RW_WIDTH), jnp.float32, -6.0, 0.5),
        'rw_w2': nrm((no, W_LORA, RW_WIDTH), 0.5 * W_LORA ** -0.5),
        'rw_a0': nrm((no, RW_WIDTH), 0.1),
        'rw_a2': nrm((no, A_LORA, RW_WIDTH), 0.5 * A_LORA ** -0.5),
        'rw_v0': gain((no, RW_WIDTH)),
        'rw_v2': nrm((no, V_LORA, RW_WIDTH), 0.5 * V_LORA ** -0.5),
        'rw_g2': nrm((no, G_LORA, RW_WIDTH), G_LORA ** -0.5),
        'rw_k_k': 0.85 + nrm((no, RW_WIDTH), 0.05),
        'rw_k_a': 1.0 + nrm((no, RW_WIDTH), 0.05),
        'rw_r_k': nrm((no, RW_HEADS, RW_HEAD_DIM), 0.1),
        'rw_lnx_w': gain((no, RW_WIDTH)),
        'rw_lnx_b': nrm((no, RW_WIDTH), 0.02),
        'sa_q_norm': gain((no, SA_HEAD_DIM)),
        'sa_k_norm': gain((no, SA_HEAD_DIM)),
        'idx_k_norm': gain((no, IDX_DIM)),
    }


def reference(x, mix_norm, ffn_norm, ffn_gate, ffn_up, ffn_down,
              ev_w_in, ev_w_out, da_q_norm, da_k_norm, da_lambda, da_subln, sc_conv,
              od_w_in, od_w_out, rw_mu, rw_w0, rw_w2, rw_a0, rw_a2, rw_v0, rw_v2, rw_g2,
              rw_k_k, rw_k_a, rw_r_k, rw_lnx_w, rw_lnx_b, sa_q_norm, sa_k_norm, idx_k_norm):
    B, T, _ = x.shape
    pos = jnp.arange(T, dtype=jnp.int32)
    k_sel = min(TOPK_MAX, T // 4)
    v_first = None
    for i in range(DEPTH):
        h = _rms(x, mix_norm[i])
        if i % 2 == 0:
            e = i // 2
            lam_init = 0.8 - 0.6 * math.exp(-0.3 * i)
            mix, v_attn = _even_mixer(h, ev_w_in[e], ev_w_out[e], da_q_norm[e], da_k_norm[e],
                                      da_lambda[e], da_subln[e], sc_conv[e], pos, lam_init)
            if v_first is None:
                v_first = v_attn
        else:
            o = i // 2
            mix = _odd_mixer(h, od_w_in[o], od_w_out[o], rw_mu[o], rw_w0[o], rw_w2[o], rw_a0[o],
                             rw_a2[o], rw_v0[o], rw_v2[o], rw_g2[o], rw_k_k[o], rw_k_a[o], rw_r_k[o],
                             rw_lnx_w[o], rw_lnx_b[o], sa_q_norm[o], sa_k_norm[o], idx_k_norm[o],
                             v_first, pos, k_sel)
        x = x + mix.astype(x.dtype)
        h = _rms(x, ffn_norm[i])
        ffn = (jax.nn.silu(h @ ffn_gate[i]) * (h @ ffn_up[i])) @ ffn_down[i]
        x = x + ffn.astype(x.dtype)
    return x
```

```python
import contextlib
import math
import numpy as np
import ml_dtypes
import concourse.bass as bass
import concourse.mybir as mybir
from concourse.bass_utils import run_bass_kernel_spmd

F32 = mybir.dt.float32
BF16 = mybir.dt.bfloat16
AF = mybir.ActivationFunctionType
ALU = mybir.AluOpType
AX = mybir.AxisListType

NDS = 8
D = 4096
KC = 32
TT = 512
FH = 11008
FC = 86
EV_IN = 12288
OD_IN = 10064
EPS = 1e-6


class Buf:
    def __init__(self, t, name):
        self.t = t
        self.name = name
        self.lw = None
        self.rd = {}

    def __getitem__(self, k):
        return self.t[k]


class Prog:
    def __init__(self):
        nc = self.nc = bass.Bass("TRN2", target_bir_lowering=False)
        self.es = contextlib.ExitStack()
        self.eng = dict(pe=nc.tensor, act=nc.scalar, dve=nc.vector, pool=nc.gpsimd, sp=nc.sync)
        self.semh = {}
        self.cnt = {}
        for e in self.eng:
            self.semh[e] = self.es.enter_context(nc.semaphore(f"s_{e}"))
            self.cnt[e] = 0
        self.known = {e: {} for e in self.eng}
        self.dq = {}
        for q in ("sp", "act", "pool"):
            ks = []
            for i in range(NDS):
                k = f"d_{q}{i}"
                self.semh[k] = self.es.enter_context(nc.semaphore(k))
                self.cnt[k] = 0
                ks.append(k)
            self.dq[q] = [ks, 0]
        self.nbuf = 0
        self.in_names = []
        self.cur = self.es

    def sb(self, shape, dt, name=None):
        self.nbuf += 1
        name = f"S{self.nbuf}_" + (name or "sb")
        t = self.cur.enter_context(self.nc.sbuf_tensor(name, list(shape), dt))
        return Buf(t, name)

    def ps(self, shape=(128, 512), dt=F32, name=None):
        self.nbuf += 1
        name = name or f"ps{self.nbuf}"
        t = self.es.enter_context(self.nc.psum_tensor(name, list(shape), dt))
        return Buf(t, name)

    def dram(self, name, shape, dt, kind):
        if kind == "ExternalInput":
            self.in_names.append(name)
        t = self.nc.dram_tensor(name, list(shape), dt, kind=kind)
        return Buf(t.ap(), name)

    @contextlib.contextmanager
    def phase(self):
        old = self.cur
        with contextlib.ExitStack() as st:
            self.cur = st
            yield st
            self.barrier()
        self.cur = old

    def barrier(self):
        evs = [(k, v) for k, v in self.cnt.items() if v > 0]
        for e in self.eng:
            self._wait(e, evs)

    def _wait(self, e, evs):
        kn = self.known[e]
        for (k, v) in evs:
            if kn.get(k, 0) >= v:
                continue
            self.eng[e].wait_ge(self.semh[k], v)
            kn[k] = v

    def _deps(self, reads, writes):
        evs = []
        for b in reads:
            if b.lw:
                evs.append(b.lw)
        for b in writes:
            if b.lw:
                evs.append(b.lw)
            evs.extend(b.rd.items())
        return evs

    def _record(self, ev, reads, writes):
        for b in reads:
            b.rd[ev[0]] = ev[1]
        for b in writes:
            b.lw = ev
            b.rd = {}

    def op(self, e, fn, reads=(), writes=()):
        evs = self._deps(reads, writes)
        if e == "pe":
            evs = [ev for ev in evs if ev[0] != "pe"]
        self._wait(e, evs)
        ins = fn(self.eng[e])
        self.cnt[e] += 1
        ins.then_inc(self.semh[e], 1)
        ev = (e, self.cnt[e])
        self._record(ev, reads, writes)
        return ev

    def dma(self, q, out, in_, reads=(), writes=(), **kw):
        ks, n = self.dq[q]
        k = ks[n % NDS]
        self.dq[q][1] = n + 1
        evs = self._deps(reads, writes)
        if self.cnt[k] > 0:
            evs.append((k, self.cnt[k]))
        self._wait(q, evs)
        ins = self.eng[q].dma_start(out=out, in_=in_, **kw)
        self.cnt[k] += 16
        ins.then_inc(self.semh[k], 16)
        ev = (k, self.cnt[k])
        self._record(ev, reads, writes)
        return ev


class Ctx:
    pass


KB = 8


def group_blocks(segs):
    blocks = []
    off = 0
    for si, (wb_, ap, c0, ncol) in enumerate(segs):
        o = 0
        while o < ncol:
            m = min(128, ncol - o)
            blocks.append((si, off + o, m, c0 + o))
            o += m
        off += ncol
    return blocks, off


def cast_weights(p, C, groups, nk, Wb):
    for gi, segs in enumerate(groups):
        blocks, tot = group_blocks(segs)
        for k0 in range(0, nk, KB):
            kb = min(KB, nk - k0)
            st = C.cst_f[C.ci % 2]
            bf = C.cst_b[C.ci % 2]
            off = 0
            for (wb_, ap, c0, ncol) in segs:
                p.dma("sp", st[:, 0:kb, off:off + ncol],
                      ap[k0 * 128:(k0 + kb) * 128, c0:c0 + ncol].rearrange("(c p) n -> p c n", p=128),
                      reads=[wb_], writes=[st])
                off += ncol
            if C.ci % 2 == 0:
                p.op("dve", lambda e: e.tensor_copy(bf[:, 0:kb, 0:tot], st[:, 0:kb, 0:tot]), [st], [bf])
            else:
                p.op("act", lambda e: e.copy(bf[:, 0:kb, 0:tot], st[:, 0:kb, 0:tot]), [st], [bf])
            C.ci += 1
            p.dma("pool", Wb[gi, :, k0:k0 + kb, 0:tot], bf[:, 0:kb, 0:tot], reads=[bf], writes=[Wb])


def linear(p, C, hT, nk, groups, Wb, epi):
    for gi, segs in enumerate(groups):
        banks = C.bank[(C.gpar % 2) * 4:(C.gpar % 2) * 4 + 4]
        C.gpar += 1
        blocks, tot = group_blocks(segs)
        for k0 in range(0, nk, C.wkb):
            kb = min(C.wkb, nk - k0)
            wt = C.wbig[C.wi % len(C.wbig)]
            C.wi += 1
            p.dma("sp", wt[:, 0:kb, 0:tot], Wb[gi, :, k0:k0 + kb, 0:tot], reads=[Wb], writes=[wt])
            for kk in range(kb):
                ki = k0 + kk
                for bi, (si, o, m, cc) in enumerate(blocks):
                    p.op("pe", lambda e: e.matmul(banks[bi][0:m, :], wt[:, kk, o:o + m], hT[:, ki, :],
                                                  start=(ki == 0), stop=(ki == nk - 1)), [wt, hT], [banks[bi]])
        epi(gi, blocks, banks)


def col_groups(wbuf, ap, n, gsz=512):
    return [[(wbuf, ap, c0, min(gsz, n - c0))] for c0 in range(0, n, gsz)]


def rms_tile(p, C, xt, hT, gcol):
    acc = C.bank[7]
    for c in range(KC):
        sq = C.sqb[c % 2]
        p.op("act", lambda e: e.activation(out=sq[:], in_=xt[:, c, :], func=AF.Square), [xt], [sq])
        p.op("pe", lambda e: e.matmul(acc[:], C.ones_bf[:], sq[:], start=(c == 0), stop=(c == KC - 1)),
             [C.ones_bf, sq], [acc])
    p.op("act", lambda e: e.activation(out=C.rstd[:], in_=acc[:], func=AF.Sqrt, scale=1.0 / D,
                                       bias=C.cst[:, 0:1]), [acc, C.cst], [C.rstd])
    p.op("dve", lambda e: e.reciprocal(C.rstd[:], C.rstd[:]), [C.rstd], [C.rstd])
    for c in range(KC):
        p.op("dve", lambda e: e.scalar_tensor_tensor(out=hT[:, c, :], in0=xt[:, c, :], scalar=gcol(c),
                                                     in1=C.rstd[:], op0=ALU.mult, op1=ALU.mult),
             [xt, C.rstd, C.gn], [hT])


def load_tile_fm(p, q, dst, src, t0, nchunks, step=8):
    for c0 in range(0, nchunks, step):
        p.dma(q, dst[:, c0:c0 + step, :],
              src[c0 * 128:(c0 + step) * 128, t0:t0 + TT].rearrange("(c p) t -> p c t", p=128),
              reads=[src], writes=[dst])


def store_tile_fm(p, q, dst, src, t0, nchunks, step=8):
    for c0 in range(0, nchunks, step):
        p.dma(q, dst[c0 * 128:(c0 + step) * 128, t0:t0 + TT].rearrange("(c p) t -> p c t", p=128),
              src[:, c0:c0 + step, :], reads=[src], writes=[dst])


def phase_in(p, C, l, resid, Win, nin, T):
    groups = col_groups(Win, Win.t, nin)
    with p.phase():
        C.cst_f = [p.sb([128, KB, 512], F32, f"cf{i}") for i in range(2)]
        C.cst_b = [p.sb([128, KB, 512], BF16, f"cb{i}") for i in range(2)]
        cast_weights(p, C, groups, KC, C.Wb_in)
    with p.phase():
        C.wkb = 8
        C.wbig = [p.sb([128, 8, 512], BF16, f"wbig{i}") for i in range(3)]
        xt = p.sb([128, KC, TT], F32, "A_xt")
        hTs = [p.sb([128, KC, TT], BF16, f"A_hT{i}") for i in range(2)]
        zst = [p.sb([128, TT], F32, f"A_z{i}") for i in range(4)]
        zi = [0]
        for tt in range(T // TT):
            t0 = tt * TT
            hT = hTs[tt % 2]
            load_tile_fm(p, "act", xt, resid, t0, KC)
            rms_tile(p, C, xt, hT, lambda c: C.gn[:, l, c:c + 1])

            def epi(gi, blocks, banks):
                for bi, (si, o, m, cc) in enumerate(blocks):
                    z = zst[zi[0] % 4]
                    zi[0] += 1
                    if bi % 2 == 0:
                        p.op("act", lambda e: e.copy(z[0:m, :], banks[bi][0:m, :]), [banks[bi]], [z])
                    else:
                        p.op("dve", lambda e: e.tensor_copy(z[0:m, :], banks[bi][0:m, :]), [banks[bi]], [z])
                    p.dma("act", C.zT[cc:cc + m, t0:t0 + TT], z[0:m, :], reads=[z], writes=[C.zT])

            linear(p, C, hT, KC, groups, C.Wb_in, epi)


def phase_out_ffn(p, C, l, src, dst, Wout, Wg, Wu, Wd, T):
    g_out = col_groups(Wout, Wout.t, D)
    ggroups = [[(Wg, Wg.t, c0, 256), (Wu, Wu.t, c0, 256)] for c0 in range(0, FH, 256)]
    g_dn = col_groups(Wd, Wd.t, D)
    with p.phase():
        C.cst_f = [p.sb([128, KB, 512], F32, f"cf{i}") for i in range(2)]
        C.cst_b = [p.sb([128, KB, 512], BF16, f"cb{i}") for i in range(2)]
        cast_weights(p, C, g_out, KC, C.Wb_out)
        cast_weights(p, C, ggroups, KC, C.Wb_gu)
        cast_weights(p, C, g_dn, FC, C.Wb_dn)
    with p.phase():
        C.wkb = 4
        C.wbig = [p.sb([128, 4, 512], BF16, f"wbig{i}") for i in range(3)]
        xt = p.sb([128, KC, TT], F32, "C_xt")
        mh = p.sb([128, KC, TT], BF16, "C_mh")
        act = p.sb([128, FC, TT], BF16, "C_act")
        tmp = [p.sb([128, TT], F32, f"C_tmp{i}") for i in range(2)]
        ti = [0]
        for tt in range(T // TT):
            t0 = tt * TT
            load_tile_fm(p, "act", xt, src, t0, KC)
            load_tile_fm(p, "act", mh, C.mT, t0, KC)

            def epi_add(gi, blocks, banks):
                for bi, (si, o, m, cc) in enumerate(blocks):
                    c = cc // 128
                    p.op("dve", lambda e: e.tensor_tensor(out=xt[:, c, :], in0=xt[:, c, :], in1=banks[bi][:],
                                                          op=ALU.add), [xt, banks[bi]], [xt])

            linear(p, C, mh, KC, g_out, C.Wb_out, epi_add)
            rms_tile(p, C, xt, mh, lambda c: C.gn[:, 4 + l, c:c + 1])

            def epi_glu(gi, blocks, banks):
                nb = len(blocks) // 2
                for bi in range(nb):
                    cc = blocks[bi][3]
                    c = cc // 128
                    t_ = tmp[ti[0] % 2]
                    ti[0] += 1
                    p.op("act", lambda e: e.activation(out=t_[:], in_=banks[bi][:], func=AF.Silu), [banks[bi]], [t_])
                    p.op("dve", lambda e: e.tensor_tensor(out=act[:, c, :], in0=t_[:], in1=banks[nb + bi][:],
                                                          op=ALU.mult), [t_, banks[nb + bi]], [act])

            linear(p, C, mh, KC, ggroups, C.Wb_gu, epi_glu)
            linear(p, C, act, FC, g_dn, C.Wb_dn, epi_add)
            store_tile_fm(p, "act", dst, xt, t0, KC)


def norm_rope(p, C, src, dst, gcol, bd, rm, cos, sin, rs, tA, tB, T, gsz, do_norm=True):
    acc = C.bank[7]
    nsl = T // 512
    if do_norm:
        for j in range(nsl):
            sl = slice(j * 512, (j + 1) * 512)
            sq = C.sq32[j % 2]
            p.op("act", lambda e: e.activation(out=sq[:], in_=src[:, sl], func=AF.Square), [src], [sq])
            p.op("pe", lambda e: e.matmul(acc[:], bd[:], sq[:], start=True, stop=True), [bd, sq], [acc])
            p.op("act", lambda e: e.activation(out=rs[:, sl], in_=acc[:], func=AF.Sqrt, scale=1.0 / gsz,
                                               bias=C.cst[:, 0:1]), [acc, C.cst], [rs])
        p.op("dve", lambda e: e.reciprocal(rs[:, 0:T], rs[:, 0:T]), [rs], [rs])
        p.op("dve", lambda e: e.scalar_tensor_tensor(out=tA[:, 0:T], in0=src[:, 0:T], scalar=gcol, in1=rs[:, 0:T],
                                                     op0=ALU.mult, op1=ALU.mult), [src, rs, C.sm], [tA])
        cur = tA
    else:
        cur = src
    for j in range(nsl):
        sl = slice(j * 512, (j + 1) * 512)
        p.op("pe", lambda e: e.matmul(acc[:], rm[:], cur[:, sl], start=True, stop=True), [rm, cur], [acc])
        p.op("dve", lambda e: e.tensor_tensor(out=tB[:, sl], in0=acc[:], in1=sin[:, sl], op=ALU.mult),
             [acc, sin], [tB])
    p.op("dve", lambda e: e.tensor_tensor(out=tA[:, 0:T], in0=cur[:, 0:T], in1=cos[:, 0:T], op=ALU.mult),
         [cur, cos], [tA])
    p.op("dve", lambda e: e.tensor_tensor(out=dst[:, 0:T], in0=tA[:, 0:T], in1=tB[:, 0:T], op=ALU.add),
         [tA, tB], [dst])


def mixer_even(p, C, e, T, save_vfirst):
    lam_init = 0.8 - 0.6 * math.exp(-0.3 * (2 * e))
    with p.phase():
        C.sq32 = [p.sb([128, 512], F32, f"sq32{i}") for i in range(2)]
        cos = p.sb([128, T], F32, "E_cos")
        sin = p.sb([128, T], F32, "E_sin")
        qt = p.sb([128, T], F32, "E_qt")
        kt = p.sb([128, T], F32, "E_kt")
        vT = p.sb([128, T], F32, "E_vT")
        tA = p.sb([128, T], F32, "E_tA")
        tB = p.sb([128, T], F32, "E_tB")
        rs = p.sb([128, T], F32, "E_rs")
        qr = p.sb([128, T], BF16, "E_qr")
        kr = p.sb([128, T], BF16, "E_kr")
        vtok = p.sb([128, T // 128, 128], BF16, "E_vtok")
        pts = [p.sb([128, 512], BF16, f"E_pt{i}") for i in range(3)]
        msk = p.sb([128, 4, 512], BF16, "E_msk")
        o0 = p.sb([128, 512], F32, "E_o0")
        o1 = p.sb([128, 512], F32, "E_o1")
        rl = p.sb([128, 512], F32, "E_rl")
        ob = [p.sb([128, 512], BF16, f"E_ob{i}") for i in range(2)]
        lam = p.sb([128, 2], F32, "E_lam")
        lt = p.sb([1, 256], F32, "E_lt")
        lt2 = p.sb([1, 128], F32, "E_lt2")
        ls = p.sb([1, 2], F32, "E_ls")
        p.dma("act", cos[:], C.rope64c.t[:, 0:T], reads=[C.rope64c], writes=[cos])
        p.dma("act", sin[:], C.rope64s.t[:, 0:T], reads=[C.rope64s], writes=[sin])
        p.dma("act", msk[:], C.dmask.t[:, :, :], reads=[C.dmask], writes=[msk])
        if save_vfirst:
            for i in range(4):
                p.dma("act", C.vfirst[i * 512:(i + 1) * 512, :], C.zT[4096 + i * 512:4096 + (i + 1) * 512, 0:T],
                      reads=[C.zT], writes=[C.vfirst])
        p.dma("act", lt[:], C.dalam.t[e:e + 1, :], reads=[C.dalam], writes=[lt])
        p.op("dve", lambda en: en.tensor_tensor(out=lt2[:, 0:64], in0=lt[:, 0:64], in1=lt[:, 64:128], op=ALU.mult),
             [lt], [lt2])
        p.op("dve", lambda en: en.tensor_tensor(out=lt2[:, 64:128], in0=lt[:, 128:192], in1=lt[:, 192:256],
                                                op=ALU.mult), [lt], [lt2])
        p.op("dve", lambda en: en.reduce_sum(out=ls[:], in_=lt2[:].rearrange("a (b c) -> a b c", b=2), axis=AX.X),
             [lt2], [ls])
        p.op("act", lambda en: en.activation(out=ls[:], in_=ls[:], func=AF.Exp), [ls], [ls])
        p.op("dve", lambda en: en.scalar_tensor_tensor(out=ls[:, 0:1], in0=ls[:, 1:2], scalar=-lam_init,
                                                       in1=ls[:, 0:1], op0=ALU.add, op1=ALU.subtract), [ls], [ls])
        acc = C.bank[7]
        p.op("pe", lambda en: en.matmul(acc[:, 0:1], C.ones_f[0:1, :], ls[0:1, 0:1], start=True, stop=True),
             [C.ones_f, ls], [acc])
        p.op("dve", lambda en: en.tensor_copy(lam[:, 0:1], acc[:, 0:1]), [acc], [lam])
        p.op("dve", lambda en: en.tensor_scalar(out=lam[:, 1:2], in0=C.sm[:, 4 + e:5 + e], scalar1=1.0 - lam_init,
                                                scalar2=None, op0=ALU.mult), [C.sm], [lam])
        nqg = T // 512
        pti = 0
        for h in range(16):
            p.dma("act", qt[:], C.zT[h * 128:(h + 1) * 128, 0:T], reads=[C.zT], writes=[qt])
            p.dma("act", kt[:], C.zT[2048 + h * 128:2048 + (h + 1) * 128, 0:T], reads=[C.zT], writes=[kt])
            p.dma("act", vT[:], C.zT[4096 + h * 128:4096 + (h + 1) * 128, 0:T], reads=[C.zT], writes=[vT])
            norm_rope(p, C, qt, qr, C.sm[:, e:e + 1], C.bd64, C.rm64, cos, sin, rs, tA, tB, T, 64)
            norm_rope(p, C, kt, kr, C.sm[:, 2 + e:3 + e], C.bd64, C.rm64, cos, sin, rs, tA, tB, T, 64)
            for j4 in range(T // 512):
                for jj in range(4):
                    j = j4 * 4 + jj
                    p.op("pe", lambda en: en.transpose(acc[:, jj * 128:(jj + 1) * 128], vT[:, j * 128:(j + 1) * 128],
                                                       C.ident[:]), [vT, C.ident], [acc])
                p.op("act", lambda en: en.copy(vtok[:, j4 * 4:j4 * 4 + 4, :],
                                               acc[:].rearrange("p (a b) -> p a b", a=4)), [acc], [vtok])
            for g in range(nqg):
                qs = slice(g * 512, (g + 1) * 512)
                for c in range(2):
                    Oc = C.bank[3 + 2 * c]
                    Lc = C.bank[4 + 2 * c]
                    ps_ = slice(c * 64, (c + 1) * 64)
                    nj = 4 * g + 4
                    for j in range(nj):
                        sb_ = C.bank[pti % 3]
                        pt = pts[pti % 3]
                        pti += 1
                        p.op("pe", lambda en: en.matmul(sb_[:], kr[ps_, j * 128:(j + 1) * 128], qr[ps_, qs],
                                                        start=True, stop=True), [kr, qr], [sb_])
                        p.op("act", lambda en: en.activation(out=pt[:], in_=sb_[:], func=AF.Exp, scale=0.125),
                             [sb_], [pt])
                        if j >= 4 * g:
                            p.op("dve", lambda en: en.tensor_tensor(out=pt[:], in0=pt[:], in1=msk[:, j - 4 * g, :],
                                                                    op=ALU.mult), [pt, msk], [pt])
                        p.op("pe", lambda en: en.matmul(Oc[:], vtok[:, j, :], pt[:], start=(j == 0),
                                                        stop=(j == nj - 1)), [vtok, pt], [Oc])
                        p.op("pe", lambda en: en.matmul(Lc[:], C.ones_bf[:], pt[:], start=(j == 0),
                                                        stop=(j == nj - 1)), [C.ones_bf, pt], [Lc])
                p.op("dve", lambda en: en.reciprocal(rl[:], C.bank[4][:]), [C.bank[4]], [rl])
                p.op("dve", lambda en: en.tensor_tensor(out=o0[:], in0=C.bank[3][:], in1=rl[:], op=ALU.mult),
                     [C.bank[3], rl], [o0])
                p.op("dve", lambda en: en.reciprocal(rl[:], C.bank[6][:]), [C.bank[6]], [rl])
                p.op("dve", lambda en: en.tensor_tensor(out=o1[:], in0=C.bank[5][:], in1=rl[:], op=ALU.mult),
                     [C.bank[5], rl], [o1])
                p.op("dve", lambda en: en.scalar_tensor_tensor(out=o0[:], in0=o1[:], scalar=lam[:, 0:1], in1=o0[:],
                                                               op0=ALU.mult, op1=ALU.add), [o1, o0, lam], [o0])
                sq = C.sq32[0]
                p.op("act", lambda en: en.activation(out=sq[:], in_=o0[:], func=AF.Square), [o0], [sq])
                p.op("pe", lambda en: en.matmul(acc[:], C.ones_f[:], sq[:], start=True, stop=True),
                     [C.ones_f, sq], [acc])
                p.op("act", lambda en: en.activation(out=rl[:], in_=acc[:], func=AF.Sqrt, scale=1.0 / 128,
                                                     bias=C.cst[:, 0:1]), [acc, C.cst], [rl])
                p.op("dve", lambda en: en.reciprocal(rl[:], rl[:]), [rl], [rl])
                obb = ob[g % 2]
                p.op("dve", lambda en: en.scalar_tensor_tensor(out=obb[:], in0=o0[:], scalar=lam[:, 1:2], in1=rl[:],
                                                               op0=ALU.mult, op1=ALU.mult), [o0, rl, lam], [obb])
                p.dma("act", C.mT[h * 128:(h + 1) * 128, qs], obb[:], reads=[obb], writes=[C.mT])
        for c in range(16):
            p.dma("act", qt[:], C.zT[6144 + c * 128:6144 + (c + 1) * 128, 0:T], reads=[C.zT], writes=[qt])
            p.dma("act", kt[:], C.zT[8192 + c * 128:8192 + (c + 1) * 128, 0:T], reads=[C.zT], writes=[kt])
            p.dma("act", vT[:], C.zT[10240 + c * 128:10240 + (c + 1) * 128, 0:T], reads=[C.zT], writes=[vT])
            w = lambda j: C.cw[:, (e * 3 + j) * 16 + c:(e * 3 + j) * 16 + c + 1]
            p.op("dve", lambda en: en.tensor_tensor(out=tA[:], in0=kt[:], in1=vT[:], op=ALU.mult), [kt, vT], [tA])
            p.op("dve", lambda en: en.tensor_scalar(out=tB[:], in0=tA[:], scalar1=w(2), scalar2=None, op0=ALU.mult),
                 [tA, C.cw], [tB])
            p.op("dve", lambda en: en.scalar_tensor_tensor(out=tB[:, 1:T], in0=tA[:, 0:T - 1], scalar=w(1),
                                                           in1=tB[:, 1:T], op0=ALU.mult, op1=ALU.add),
                 [tA, tB, C.cw], [tB])
            p.op("dve", lambda en: en.scalar_tensor_tensor(out=tB[:, 2:T], in0=tA[:, 0:T - 2], scalar=w(0),
                                                           in1=tB[:, 2:T], op0=ALU.mult, op1=ALU.add),
                 [tA, tB, C.cw], [tB])
            p.op("dve", lambda en: en.tensor_tensor(out=qr[:], in0=qt[:], in1=tB[:], op=ALU.mult), [qt, tB], [qr])
            p.dma("act", C.mT[2048 + c * 128:2048 + (c + 1) * 128, 0:T], qr[:], reads=[qr], writes=[C.mT])


SA_SCALE = 128 ** -0.5
NEG = -1.0e30


def dsa_prep(p, C, o, T):
    NB = T // 128
    with p.phase():
        C.sq32 = [p.sb([128, 512], F32, f"sq32{i}") for i in range(2)]
        c128 = p.sb([128, T], F32, "P_c128")
        s128 = p.sb([128, T], F32, "P_s128")
        c64 = p.sb([128, T], F32, "P_c64")
        s64 = p.sb([128, T], F32, "P_s64")
        qt = p.sb([128, T], F32, "P_qt")
        tA = p.sb([128, T], F32, "P_tA")
        tB = p.sb([128, T], F32, "P_tB")
        rs = p.sb([128, T], F32, "P_rs")
        ob = p.sb([128, T], BF16, "P_ob")
        vtok = p.sb([128, NB, 128], BF16, "P_vtok")
        wit = p.sb([128, NB, 16], F32, "P_wit")
        acc = C.bank[7]
        p.dma("act", c128[:], C.rope128c.t[:, 0:T], reads=[C.rope128c], writes=[c128])
        p.dma("act", s128[:], C.rope128s.t[:, 0:T], reads=[C.rope128s], writes=[s128])
        p.dma("act", c64[:], C.rope64c.t[:, 0:T], reads=[C.rope64c], writes=[c64])
        p.dma("act", s64[:], C.rope64s.t[:, 0:T], reads=[C.rope64s], writes=[s64])
        p.dma("act", qt[:], C.zT[8704:8832, 0:T], reads=[C.zT], writes=[qt])
        norm_rope(p, C, qt, ob, C.sm2[:, 2 + o:3 + o], C.ones_f, C.rm128, c128, s128, rs, tA, tB, T, 128)
        p.dma("act", C.kdr[:, 0:T], ob[:], reads=[ob], writes=[C.kdr])
        p.dma("act", qt[0:64, :], C.zT[9984:10048, 0:T], reads=[C.zT], writes=[qt])
        p.dma("act", qt[64:128, :], C.zT[9984:10048, 0:T], reads=[C.zT], writes=[qt])
        norm_rope(p, C, qt, ob, C.sm2[:, 4 + o:5 + o], C.bd64, C.rm64, c64, s64, rs, tA, tB, T, 64)
        p.dma("act", C.kir[:, 0:T], ob[0:64, :], reads=[ob], writes=[C.kir])
        p.dma("act", qt[:], C.zT[8832:8960, 0:T], reads=[C.zT], writes=[qt])
        for j4 in range(T // 512):
            for jj in range(4):
                j = j4 * 4 + jj
                p.op("pe", lambda en: en.transpose(acc[:, jj * 128:(jj + 1) * 128], qt[:, j * 128:(j + 1) * 128],
                                                   C.ident[:]), [qt, C.ident], [acc])
            p.op("act", lambda en: en.copy(vtok[:, j4 * 4:j4 * 4 + 4, :], acc[:].rearrange("p (a b) -> p a b", a=4)),
                 [acc], [vtok])
        p.dma("act", C.vtk[:, 0:NB, :], vtok[:], reads=[vtok], writes=[C.vtk])
        p.dma("act", tA[0:16, :], C.zT[10048:10064, 0:T], reads=[C.zT], writes=[tA])
        for j in range(NB):
            p.op("pe", lambda en: en.transpose(acc[:, 0:16], tA[0:16, j * 128:(j + 1) * 128], C.ident[0:16, 0:16]),
                 [tA, C.ident], [acc])
            p.op("act", lambda en: en.mul(wit[:, j, :], acc[:, 0:16], 1.0 / 32.0), [acc], [wit])
        p.dma("act", C.wtk[:, 0:NB, :], wit[:], reads=[wit], writes=[C.wtk])
        for h in range(16):
            p.dma("act", qt[:], C.zT[6656 + h * 128:6656 + (h + 1) * 128, 0:T], reads=[C.zT], writes=[qt])
            norm_rope(p, C, qt, ob, C.sm2[:, o:o + 1], C.ones_f, C.rm128, c128, s128, rs, tA, tB, T, 128)
            p.dma("act", C.qdr[h * 128:(h + 1) * 128, 0:T], ob[:], reads=[ob], writes=[C.qdr])
        for c in range(8):
            p.dma("act", qt[:], C.zT[8960 + c * 128:8960 + (c + 1) * 128, 0:T], reads=[C.zT], writes=[qt])
            norm_rope(p, C, qt, ob, None, None, C.rm64, c64, s64, rs, tA, tB, T, 64, do_norm=False)
            p.dma("act", C.qir[c * 128:(c + 1) * 128, 0:T], ob[:], reads=[ob], writes=[C.qir])


def dsa_attn(p, C, o, T):
    NB = T // 128
    with p.phase():
        kdr = p.sb([128, T], BF16, "D_kdr")
        kir = p.sb([128, T], BF16, "D_kir")
        vtok = p.sb([128, NB, 128], BF16, "D_vtok")
        wit = p.sb([128, NB, 16], F32, "D_wit")
        tri = p.sb([128, 128], F32, "D_tri")
        Iacc = p.sb([128, T], F32, "D_I")
        work = p.sb([128, T], F32, "D_w")
        maskT = p.sb([128, NB, 128], BF16, "D_mT")
        m8 = p.sb([128, 8], F32, "D_m8")
        qd2 = [p.sb([128, 16, 128], BF16, f"D_qd{i}") for i in range(2)]
        qi2 = [p.sb([128, 16, 128], BF16, f"D_qi{i}") for i in range(2)]
        rl = [p.sb([128, 512], F32, f"D_rl{i}") for i in range(2)]
        pts = [p.sb([128, 512], BF16, f"D_pt{i}") for i in range(3)]
        osb = [p.sb([128, 512], BF16, f"D_os{i}") for i in range(2)]
        rcp = p.sb([128, 512], F32, "D_rcp")
        acc = C.bank[7]
        p.dma("act", kdr[:], C.kdr[:, 0:T], reads=[C.kdr], writes=[kdr])
        p.dma("act", kir[0:64, :], C.kir[:, 0:T], reads=[C.kir], writes=[kir])
        p.dma("act", vtok[:], C.vtk[:, 0:NB, :], reads=[C.vtk], writes=[vtok])
        p.dma("act", wit[:], C.wtk[:, 0:NB, :], reads=[C.wtk], writes=[wit])
        p.dma("act", tri[:], C.trid.t[:, :], reads=[C.trid], writes=[tri])
        bi = 0
        pti = 0
        for qb in range(NB):
            q0 = qb * 128
            nb = qb + 1
            L = nb * 128
            qd = qd2[qb % 2]
            qi = qi2[qb % 2]
            p.dma("act", qd[:], C.qdr[:, q0:q0 + 128].rearrange("(h d) q -> d h q", d=128), reads=[C.qdr], writes=[qd])
            p.dma("act", qi[0:64, :, :], C.qir[:, q0:q0 + 128].rearrange("(h d) q -> d h q", d=64),
                  reads=[C.qir], writes=[qi])
            for s0 in range(0, L, 512):
                w = min(512, L - s0)
                for h in range(16):
                    bk = C.bank[bi % 3]
                    r_ = rl[bi % 2]
                    bi += 1
                    p.op("pe", lambda en: en.matmul(bk[:, 0:w], qi[0:64, h, :], kir[0:64, s0:s0 + w], start=True,
                                                    stop=True), [qi, kir], [bk])
                    p.op("act", lambda en: en.activation(out=r_[:, 0:w], in_=bk[:, 0:w], func=AF.Relu), [bk], [r_])
                    if h == 0:
                        p.op("dve", lambda en: en.tensor_scalar(out=Iacc[:, s0:s0 + w], in0=r_[:, 0:w],
                                                                scalar1=wit[:, qb, 0:1], scalar2=None, op0=ALU.mult),
                             [r_, wit], [Iacc])
                    else:
                        p.op("dve", lambda en: en.scalar_tensor_tensor(out=Iacc[:, s0:s0 + w], in0=r_[:, 0:w],
                                                                       scalar=wit[:, qb, h:h + 1],
                                                                       in1=Iacc[:, s0:s0 + w], op0=ALU.mult,
                                                                       op1=ALU.add), [r_, wit, Iacc], [Iacc])
            p.op("dve", lambda en: en.tensor_tensor(out=Iacc[:, q0:q0 + 128], in0=Iacc[:, q0:q0 + 128], in1=tri[:],
                                                    op=ALU.add), [Iacc, tri], [Iacc])
            if qb >= 2:
                p.op("act", lambda en: en.copy(work[:, 0:L], Iacc[:, 0:L]), [Iacc], [work])
                for r in range(32):
                    p.op("dve", lambda en: en.max(out=m8[:], in_=work[:, 0:L]), [work], [m8])
                    if r < 31:
                        p.op("dve", lambda en: en.match_replace(out=work[:, 0:L], in_to_replace=m8[:],
                                                                in_values=work[:, 0:L], imm_value=-3.0e38),
                             [work, m8], [work])
                thr = m8[:, 7:8]
                thr_b = [m8]
            else:
                thr = C.cst[:, 1:2]
                thr_b = [C.cst]
            p.op("dve", lambda en: en.tensor_scalar(out=work[:, 0:L], in0=Iacc[:, 0:L], scalar1=thr, scalar2=None,
                                                    op0=ALU.is_ge), [Iacc] + thr_b, [work])
            for j4 in range(0, nb, 4):
                n4 = min(4, nb - j4)
                for jj in range(n4):
                    p.op("pe", lambda en: en.transpose(acc[:, jj * 128:(jj + 1) * 128],
                                                       work[:, (j4 + jj) * 128:(j4 + jj + 1) * 128], C.ident[:]),
                         [work, C.ident], [acc])
                p.op("act", lambda en: en.copy(maskT[:, j4:j4 + n4, :],
                                               acc[:, 0:n4 * 128].rearrange("p (a b) -> p a b", a=n4)), [acc], [maskT])
            for half in range(2):
                for j in range(nb):
                    for quad in range(2):
                        h0 = half * 8 + quad * 4
                        Oc = C.bank[3 + 2 * quad]
                        Lc = C.bank[4 + 2 * quad]
                        sb_ = C.bank[pti % 3]
                        pt = pts[pti % 3]
                        pti += 1
                        p.op("pe", lambda en: en.matmul(sb_[:], kdr[:, j * 128:(j + 1) * 128], qd[:, h0:h0 + 4, :],
                                                        start=True, stop=True), [kdr, qd], [sb_])
                        p.op("act", lambda en: en.activation(out=pt[:], in_=sb_[:], func=AF.Exp, scale=SA_SCALE),
                             [sb_], [pt])
                        p.op("dve", lambda en: en.tensor_tensor(
                            out=pt[:].rearrange("p (a b) -> p a b", a=4), in0=pt[:].rearrange("p (a b) -> p a b", a=4),
                            in1=maskT[:, j:j + 1, :].to_broadcast([128, 4, 128]), op=ALU.mult), [pt, maskT], [pt])
                        p.op("pe", lambda en: en.matmul(Oc[:], vtok[:, j, :], pt[:], start=(j == 0),
                                                        stop=(j == nb - 1)), [vtok, pt], [Oc])
                        p.op("pe", lambda en: en.matmul(Lc[:], C.ones_bf[:], pt[:], start=(j == 0),
                                                        stop=(j == nb - 1)), [C.ones_bf, pt], [Lc])
                for quad in range(2):
                    h0 = half * 8 + quad * 4
                    Oc = C.bank[3 + 2 * quad]
                    Lc = C.bank[4 + 2 * quad]
                    ob = osb[quad]
                    p.op("dve", lambda en: en.reciprocal(rcp[:], Lc[:]), [Lc], [rcp])
                    p.op("dve", lambda en: en.tensor_tensor(out=ob[:], in0=Oc[:], in1=rcp[:], op=ALU.mult),
                         [Oc, rcp], [ob])
                    p.dma("act", C.mT[2048 + h0 * 128:2048 + (h0 + 4) * 128, q0:q0 + 128].rearrange(
                        "(h d) q -> d h q", d=128), ob[:].rearrange("p (a b) -> p a b", a=4), reads=[ob], writes=[C.mT])


SEG = 512
CDEC = math.exp(-0.5)
NPRM = 181


def rwkv_part(p, C, o, T):
    import os
    RWS = int(os.environ.get("RW_STOP", "99"))
    seg = min(SEG, T)
    nseg = T // seg
    NCH = seg // 64
    NM = NCH * 2
    NG = NM // 4
    W2, A2, V2, G2 = C.lora[o]
    with p.phase():
        rwp = p.sb([128, NPRM], F32, "R_rwp")
        omm = p.sb([128, 53], F32, "R_omm")
        omka = p.sb([128, 16], F32, "R_omka")
        w2b = p.sb([128, 2048], BF16, "R_w2b")
        a2b = p.sb([128, 2048], BF16, "R_a2b")
        v2b = p.sb([128, 2048], BF16, "R_v2b")
        g2b = p.sb([128, 2, 2048], BF16, "R_g2b")
        stg = p.sb([128, 2048], F32, "R_stg")
        cm = p.sb([128, seg], F32, "R_cm")
        m320 = p.sb([128, 512], F32, "R_m320")
        twd = p.sb([128, T], BF16, "R_twd")
        sad = p.sb([128, T], BF16, "R_sad")
        svd = p.sb([128, T], BF16, "R_svd")
        sgd = p.sb([128, 2, T], BF16, "R_sgd")
        xr = p.sb([128, seg + 1], F32, "R_xr")
        names = ["rS", "kS", "vS", "sw", "aG", "t1", "t2", "kk", "Ls", "E1", "E2", "E3", "bt", "kt", "gT", "y", "ym"]
        tl = {n: p.sb([128, seg], F32, "R_" + n) for n in names}
        rS, kS, vS, sw, aG, t1, t2, kk, Ls, E1, E2, E3, bt32, kt32, gT, y, ym = [tl[n] for n in names]
        AR = p.sb([128, NCH, 128], BF16, "R_AR")
        BK = p.sb([128, NCH, 128], BF16, "R_BK")
        BT = p.sb([128, NCH, 128], BF16, "R_BT")
        KT = p.sb([128, NCH, 128], BF16, "R_KT")
        VT = p.sb([128, NCH, 128], BF16, "R_VT")
        VA0 = p.sb([128, NCH, 128], BF16, "R_VA0")
        V0B = p.sb([128, NCH, 128], BF16, "R_V0B")
        Mm = p.sb([128, NCH, 2, 256], BF16, "R_Mm")
        XXg = [p.sb([128, 4, 2, 64], F32, f"R_XX{g}") for g in range(NG)]
        RRg = [p.sb([128, 4, 2, 64], F32, f"R_RR{g}") for g in range(NG)]
        X2g = [p.sb([128, 4, 2, 64], F32, f"R_X2{g}") for g in range(NG)]
        EE = p.sb([128, 384], BF16, "R_EE")
        Eb = p.sb([128, 128], BF16, "R_Eb")
        Ff = p.sb([128, 128], F32, "R_Ff")
        ST = p.sb([128, 128], F32, "R_ST")
        tS = p.sb([128, 128], F32, "R_tS")
        STb = p.sb([128, 128], BF16, "R_STb")
        ob = [p.sb([128, seg], BF16, f"R_ob{i}") for i in range(2)]
        vb = p.sb([128, seg], BF16, "R_vb")
        acc = C.bank[7]

        p.dma("act", rwp[:], C.rwp.t[:, o, :], reads=[C.rwp], writes=[rwp])
        p.dma("act", cm[:], C.cmd.t[:, 0:seg], reads=[C.cmd], writes=[cm])
        p.dma("act", m320[0:64, 0:320], C.m320d.t[:, :], reads=[C.m320d], writes=[m320])
        p.op("dve", lambda en: en.tensor_scalar(out=omm[:], in0=rwp[:, 0:53], scalar1=-1.0, scalar2=1.0,
                                                op0=ALU.mult, op1=ALU.add), [rwp], [omm])
        p.op("dve", lambda en: en.tensor_scalar(out=omka[:], in0=rwp[:, 117:133], scalar1=-1.0, scalar2=1.0,
                                                op0=ALU.mult, op1=ALU.add), [rwp], [omka])
        for (wsrc, nr, dstb) in ((W2, 96, w2b), (A2, 96, a2b), (V2, 64, v2b)):
            p.dma("act", stg[0:nr, :], wsrc.t[:, :], reads=[wsrc], writes=[stg])
            p.op("pool", lambda en: en.tensor_copy(dstb[0:nr, :], stg[0:nr, :]), [stg], [dstb])
        for c2 in range(2):
            p.dma("act", stg[:, :], G2.t[c2 * 128:(c2 + 1) * 128, :], reads=[G2], writes=[stg])
            p.op("pool", lambda en: en.tensor_copy(g2b[:, c2, :], stg[:, :]), [stg], [g2b])
        for t_ in (VA0, V0B, EE):
            p.op("dve", lambda en: en.memset(t_[:], 0.0), [], [t_])

        def load_shift(dst, r0, nr, mucol, sg):
            s0 = sg * seg
            if sg == 0:
                p.op("dve", lambda en: en.memset(xr[0:nr, 0:1], 0.0), [], [xr])
                p.dma("act", xr[0:nr, 1:seg + 1], C.zT[r0:r0 + nr, 0:seg], reads=[C.zT], writes=[xr])
            else:
                p.dma("act", xr[0:nr, 0:seg + 1], C.zT[r0:r0 + nr, s0 - 1:s0 + seg], reads=[C.zT], writes=[xr])
            p.op("dve", lambda en: en.tensor_scalar(out=dst[0:nr, :], in0=xr[0:nr, 1:seg + 1],
                                                    scalar1=omm[0:nr, mucol:mucol + 1], scalar2=None, op0=ALU.mult),
                 [xr, omm], [dst])
            p.op("dve", lambda en: en.scalar_tensor_tensor(out=dst[0:nr, :], in0=xr[0:nr, 0:seg],
                                                           scalar=rwp[0:nr, mucol:mucol + 1], in1=dst[0:nr, :],
                                                           op0=ALU.mult, op1=ALU.add), [xr, rwp, dst], [dst])

        for sg in range(nseg):
            ss_ = slice(sg * seg, (sg + 1) * seg)
            load_shift(t1, 6144, 96, 48, sg)
            p.op("act", lambda en: en.activation(out=twd[0:96, ss_], in_=t1[0:96, :], func=AF.Tanh), [t1], [twd])
            load_shift(t1, 6240, 96, 49, sg)
            p.op("act", lambda en: en.copy(sad[0:96, ss_], t1[0:96, :]), [t1], [sad])
            load_shift(t1, 6336, 64, 50, sg)
            p.op("act", lambda en: en.copy(svd[0:64, ss_], t1[0:64, :]), [t1], [svd])
            for c2 in range(2):
                load_shift(t1, 6400 + c2 * 128, 128, 51 + c2, sg)
                p.op("act", lambda en: en.activation(out=sgd[:, c2, ss_], in_=t1[:, :], func=AF.Sigmoid), [t1], [sgd])

        col = lambda base, c: rwp[:, base + c:base + c + 1]
        bi = 0
        if RWS == 1:
            return
        for c in range(16):
            cs = slice(c * 128, (c + 1) * 128)
            p.op("dve", lambda en: en.memset(ST[:], 0.0), [], [ST])
            p.op("dve", lambda en: en.memset(STb[:], 0.0), [], [STb])
            for sg in range(nseg):
                s0 = sg * seg
                ss_ = slice(s0, s0 + seg)
                load_shift(rS, c * 128, 128, c, sg)
                load_shift(kS, 2048 + c * 128, 128, 16 + c, sg)
                load_shift(vS, 4096 + c * 128, 128, 32 + c, sg)
                p.dma("act", t2[:], C.vfirst[cs, ss_], reads=[C.vfirst], writes=[t2])
                p.op("pe", lambda en: en.matmul(acc[:, 0:seg], w2b[0:96, cs], twd[0:96, ss_], start=True, stop=True),
                     [w2b, twd], [acc])
                p.op("act", lambda en: en.activation(out=sw[:], in_=acc[:, 0:seg], func=AF.Sigmoid,
                                                     bias=col(53, c)), [acc, rwp], [sw])
                p.op("pe", lambda en: en.matmul(acc[:, 0:seg], a2b[0:96, cs], sad[0:96, ss_], start=True, stop=True),
                     [a2b, sad], [acc])
                p.op("act", lambda en: en.activation(out=aG[:], in_=acc[:, 0:seg], func=AF.Sigmoid,
                                                     bias=col(69, c)), [acc, rwp], [aG])
                p.op("pe", lambda en: en.matmul(acc[:, 0:seg], v2b[0:64, cs], svd[0:64, ss_], start=True, stop=True),
                     [v2b, svd], [acc])
                p.op("act", lambda en: en.activation(out=t1[:], in_=acc[:, 0:seg], func=AF.Sigmoid,
                                                     bias=col(85, c)), [acc, rwp], [t1])
                p.op("dve", lambda en: en.tensor_tensor(out=t2[:], in0=t2[:], in1=vS[:], op=ALU.subtract),
                     [t2, vS], [t2])
                p.op("dve", lambda en: en.tensor_tensor(out=t2[:], in0=t2[:], in1=t1[:], op=ALU.mult), [t2, t1], [t2])
                p.op("dve", lambda en: en.tensor_tensor(out=vS[:], in0=vS[:], in1=t2[:], op=ALU.add), [vS, t2], [vS])
                for c2 in range(2):
                    p.op("pe", lambda en: en.matmul(acc[:, 0:seg], g2b[:, c2, cs], sgd[:, c2, ss_], start=(c2 == 0),
                                                    stop=(c2 == 1)), [g2b, sgd], [acc])
                p.op("act", lambda en: en.copy(gT[:], acc[:, 0:seg]), [acc], [gT])
                p.op("dve", lambda en: en.tensor_scalar(out=kk[:], in0=kS[:], scalar1=col(101, c), scalar2=None,
                                                        op0=ALU.mult), [kS, rwp], [kk])
                p.op("act", lambda en: en.activation(out=t1[:], in_=kk[:], func=AF.Square), [kk], [t1])
                p.op("pe", lambda en: en.matmul(acc[:, 0:seg], C.bd64[:], t1[:], start=True, stop=True),
                     [C.bd64, t1], [acc])
                p.op("act", lambda en: en.activation(out=t2[:], in_=acc[:, 0:seg], func=AF.Sqrt, bias=C.cst[:, 3:4]),
                     [acc, C.cst], [t2])
                p.op("dve", lambda en: en.reciprocal(t2[:], t2[:]), [t2], [t2])
                p.op("dve", lambda en: en.tensor_tensor(out=kk[:], in0=kk[:], in1=t2[:], op=ALU.mult), [kk, t2], [kk])
                p.op("dve", lambda en: en.tensor_scalar(out=t1[:], in0=aG[:], scalar1=col(117, c),
                                                        scalar2=omka[:, c:c + 1], op0=ALU.mult, op1=ALU.add),
                     [aG, rwp, omka], [t1])
                p.op("dve", lambda en: en.tensor_tensor(out=kS[:], in0=kS[:], in1=t1[:], op=ALU.mult), [kS, t1], [kS])
                p.op("dve", lambda en: en.tensor_tensor_scan(out=Ls[:], data0=cm[:], data1=sw[:], initial=0.0,
                                                             op0=ALU.mult, op1=ALU.add), [cm, sw], [Ls])
                p.op("act", lambda en: en.activation(out=E1[:], in_=Ls[:], func=AF.Exp, scale=-CDEC), [Ls], [E1])
                p.op("act", lambda en: en.activation(out=E2[:], in_=Ls[:], func=AF.Exp, scale=CDEC), [Ls], [E2])
                p.op("dve", lambda en: en.tensor_tensor(out=t1[:], in0=Ls[:], in1=sw[:], op=ALU.subtract),
                     [Ls, sw], [t1])
                p.op("act", lambda en: en.activation(out=E3[:], in_=t1[:], func=AF.Exp, scale=-CDEC), [t1], [E3])
                v3 = lambda b_: b_[:].rearrange("p (a b) -> p a b", b=64)
                p.op("dve", lambda en: en.scalar_tensor_tensor(out=AR[:, :, 0:64], in0=v3(kk), scalar=-1.0, in1=v3(E3),
                                                               op0=ALU.mult, op1=ALU.mult), [kk, E3], [AR])
                p.op("dve", lambda en: en.tensor_tensor(out=AR[:, :, 64:128], in0=v3(rS), in1=v3(E1), op=ALU.mult),
                     [rS, E1], [AR])
                p.op("dve", lambda en: en.tensor_tensor(out=t1[:], in0=kk[:], in1=aG[:], op=ALU.mult), [kk, aG], [t1])
                p.op("dve", lambda en: en.tensor_tensor(out=bt32[:], in0=t1[:], in1=E2[:], op=ALU.mult),
                     [t1, E2], [bt32])
                p.op("dve", lambda en: en.tensor_tensor(out=kt32[:], in0=kS[:], in1=E2[:], op=ALU.mult),
                     [kS, E2], [kt32])
                p.op("act", lambda en: en.copy(BK[:, :, 0:64], v3(bt32)), [bt32], [BK])
                p.op("act", lambda en: en.copy(BK[:, :, 64:128], v3(kt32)), [kt32], [BK])
                if RWS == 2:
                    return
                p.op("act", lambda en: en.copy(vb[:], vS[:]), [vS], [vb])
                for (srcf, dsts) in ((lambda ch: BK[:, ch, 0:64], (BT,)), (lambda ch: BK[:, ch, 64:128], (KT,)),
                                     (lambda ch: vb[:, ch * 64:(ch + 1) * 64], (VT, VA0, V0B))):
                    for c4 in range(0, NCH, 4):
                        bk = C.bank[bi % 3]
                        bi += 1
                        for jj in range(4):
                            ch = c4 + jj
                            p.op("pe", lambda en: en.matmul(bk[0:64, jj * 128:(jj + 1) * 128], srcf(ch),
                                                            C.ident_bf[:], start=True, stop=True),
                                 [BK, vb, C.ident_bf], [bk])
                        bv = bk[0:64, :].rearrange("p (a b) -> p a b", a=4)
                        p.op("dve", lambda en: en.tensor_copy(dsts[0][0:64, c4:c4 + 4, :], bv), [bk], [dsts[0]])
                        if len(dsts) == 3:
                            p.op("dve", lambda en: en.tensor_copy(dsts[1][0:64, c4:c4 + 4, 0:64], bv[:, :, 0:64]),
                                 [bk], [dsts[1]])
                            p.op("dve", lambda en: en.tensor_copy(dsts[2][0:64, c4:c4 + 4, 64:128], bv[:, :, 64:128]),
                                 [bk], [dsts[2]])
                if RWS == 3:
                    return
                for ch in range(NCH):
                    for hh in range(2):
                        hs = slice(hh * 64, hh * 64 + 64)
                        m = ch * 2 + hh
                        bk = C.bank[bi % 3]
                        bi += 1
                        p.op("pe", lambda en: en.matmul(bk[0:64, 0:128], BK[hs, ch, 0:64], AR[hs, ch, :], start=True,
                                                        stop=True), [BK, AR], [bk])
                        p.op("pe", lambda en: en.matmul(bk[0:64, 128:256], BK[hs, ch, 64:128], AR[hs, ch, :],
                                                        start=True, stop=True), [BK, AR], [bk])
                        p.op("pe", lambda en: en.matmul(bk[0:64, 256:320], AR[hs, ch, 0:64], BK[hs, ch, 0:64],
                                                        start=True, stop=True), [BK, AR], [bk])
                        p.op("dve", lambda en: en.tensor_tensor(out=Mm[0:64, ch, hh, :], in0=bk[0:64, 0:256],
                                                                in1=m320[0:64, 0:256], op=ALU.mult), [bk, m320], [Mm])
                        xg = XXg[m // 4]
                        p.op("dve", lambda en: en.tensor_tensor(
                            out=xg[0:64, m % 4, :, :],
                            in0=bk[0:64, :].rearrange("p (a b) -> p a b", b=256)[:, :, 0:64],
                            in1=m320[0:64, :].rearrange("p (a b) -> p a b", b=256)[:, :, 0:64], op=ALU.mult),
                            [bk, m320], [xg])
                if RWS == 4:
                    return
                Xc, Xn = XXg, X2g
                for g in range(NG):
                    p.op("dve", lambda en: en.tensor_tensor(
                        out=RRg[g][0:64].rearrange("p a b c -> p (a b) c"),
                        in0=Xc[g][0:64].rearrange("p a b c -> p (a b) c"),
                        in1=C.ident[0:64, None, 0:64].to_broadcast([64, 8, 64]), op=ALU.add), [Xc[g], C.ident], [RRg[g]])
                for st_ in range(5):
                    for g in range(NG):
                        bk = C.bank[bi % 3]
                        bi += 1
                        for mm in range(4):
                            p.op("pe", lambda en: en.matmul(bk[0:64, mm * 128:mm * 128 + 64], Xc[g][0:64, mm, 1, :],
                                                            Xc[g][0:64, mm, 0, :], start=True, stop=True), [Xc[g]], [bk])
                            p.op("pe", lambda en: en.matmul(bk[0:64, mm * 128 + 64:mm * 128 + 128],
                                                            Xc[g][0:64, mm, 0, :], Xc[g][0:64, mm, 1, :], start=True,
                                                            stop=True), [Xc[g]], [bk])
                        p.op("act", lambda en: en.copy(Xn[g][0:64].rearrange("p a b c -> p (a b c)"), bk[0:64, :]),
                             [bk], [Xn[g]])
                    for g in range(NG):
                        bk = C.bank[bi % 3]
                        bi += 1
                        for mm in range(4):
                            p.op("pe", lambda en: en.matmul(bk[0:64, mm * 128:mm * 128 + 64], RRg[g][0:64, mm, 1, :],
                                                            Xn[g][0:64, mm, 0, :], start=True, stop=True),
                                 [RRg[g], Xn[g]], [bk])
                            p.op("pe", lambda en: en.matmul(bk[0:64, mm * 128 + 64:mm * 128 + 128],
                                                            Xn[g][0:64, mm, 0, :], RRg[g][0:64, mm, 1, :], start=True,
                                                            stop=True), [RRg[g], Xn[g]], [bk])
                        p.op("dve", lambda en: en.tensor_tensor(out=RRg[g][0:64].rearrange("p a b c -> p (a b c)"),
                                                                in0=RRg[g][0:64].rearrange("p a b c -> p (a b c)"),
                                                                in1=bk[0:64, :], op=ALU.add), [RRg[g], bk], [RRg[g]])
                    Xc, Xn = Xn, Xc
                if RWS == 5:
                    return
                bF, bE, bU, bY = C.bank[3], C.bank[4], C.bank[5], C.bank[6]
                for ch in range(NCH):
                    mA, mB = ch * 2, ch * 2 + 1
                    p.op("pe", lambda en: en.matmul(bF[0:64, 0:128], AR[:, ch, 0:64], STb[:], start=True, stop=False),
                         [AR, STb], [bF])
                    p.op("pe", lambda en: en.matmul(bF[0:64, 0:128], Mm[0:64, ch, 0, 128:192], VA0[0:64, ch, :],
                                                    start=False, stop=False), [Mm, VA0], [bF])
                    p.op("pe", lambda en: en.matmul(bF[0:64, 0:128], Mm[0:64, ch, 1, 128:192], V0B[0:64, ch, :],
                                                    start=False, stop=True), [Mm, V0B], [bF])
                    p.op("act", lambda en: en.copy(Ff[0:64, :], bF[0:64, 0:128]), [bF], [Ff])
                    for hh, m in ((0, mA), (1, mB)):
                        p.op("pe", lambda en: en.matmul(bE[0:64, hh * 64:hh * 64 + 64], RRg[m // 4][0:64, m % 4, 0, :],
                                                        Ff[0:64, hh * 64:hh * 64 + 64], start=True, stop=True),
                             [RRg[m // 4], Ff], [bE])
                    p.op("dve", lambda en: en.tensor_copy(Eb[0:64, :], bE[0:64, 0:128]), [bE], [Eb])
                    p.op("dve", lambda en: en.tensor_copy(
                        EE[0:64, :].rearrange("p (a b) -> p a b", b=192)[:, :, 0:64],
                        bE[0:64, 0:128].rearrange("p (a b) -> p a b", b=64)), [bE], [EE])
                    ys = slice(ch * 64, ch * 64 + 64)
                    p.op("pe", lambda en: en.matmul(bY[:, ys], STb[:], AR[:, ch, 64:128], start=True, stop=False),
                         [STb, AR], [bY])
                    p.op("pe", lambda en: en.matmul(bY[:, ys], EE[0:64, 0:128], Mm[0:64, ch, 0, 64:128], start=False,
                                                    stop=False), [EE, Mm], [bY])
                    p.op("pe", lambda en: en.matmul(bY[:, ys], EE[0:64, 128:256], Mm[0:64, ch, 1, 64:128], start=False,
                                                    stop=False), [EE, Mm], [bY])
                    p.op("pe", lambda en: en.matmul(bY[:, ys], VA0[0:64, ch, :], Mm[0:64, ch, 0, 192:256], start=False,
                                                    stop=False), [VA0, Mm], [bY])
                    p.op("pe", lambda en: en.matmul(bY[:, ys], V0B[0:64, ch, :], Mm[0:64, ch, 1, 192:256], start=False,
                                                    stop=True), [V0B, Mm], [bY])
                    p.op("pe", lambda en: en.matmul(bU[:, 0:128], BT[0:64, ch, :], Eb[0:64, :], start=True, stop=False),
                         [BT, Eb], [bU])
                    p.op("pe", lambda en: en.matmul(bU[:, 0:128], KT[0:64, ch, :], VT[0:64, ch, :], start=False,
                                                    stop=True), [KT, VT], [bU])
                    p.op("dve", lambda en: en.tensor_tensor(out=tS[:], in0=bU[:, 0:128], in1=ST[:], op=ALU.add),
                         [bU, ST], [tS])
                    pc = E1[:, ch * 64 + 63:ch * 64 + 64]
                    p.op("dve", lambda en: en.scalar_tensor_tensor(out=ST[:], in0=tS[:], scalar=pc, in1=C.bd64[:],
                                                                   op0=ALU.mult, op1=ALU.mult), [tS, E1, C.bd64], [ST])
                    p.op("act", lambda en: en.copy(STb[:], ST[:]), [ST], [STb])
                p.op("act", lambda en: en.copy(y[:], bY[:, 0:seg]), [bY], [y])
                if RWS == 6:
                    return
                p.op("pe", lambda en: en.matmul(acc[:, 0:seg], C.bd64[:], y[:], start=True, stop=True),
                     [C.bd64, y], [acc])
                p.op("dve", lambda en: en.scalar_tensor_tensor(out=ym[:], in0=acc[:, 0:seg], scalar=-1.0 / 64, in1=y[:],
                                                               op0=ALU.mult, op1=ALU.add), [acc, y], [ym])
                p.op("act", lambda en: en.activation(out=t1[:], in_=ym[:], func=AF.Square), [ym], [t1])
                p.op("pe", lambda en: en.matmul(acc[:, 0:seg], C.bd64[:], t1[:], start=True, stop=True),
                     [C.bd64, t1], [acc])
                p.op("act", lambda en: en.activation(out=t2[:], in_=acc[:, 0:seg], func=AF.Sqrt, scale=1.0 / 64,
                                                     bias=C.cst[:, 2:3]), [acc, C.cst], [t2])
                p.op("dve", lambda en: en.reciprocal(t2[:], t2[:]), [t2], [t2])
                p.op("dve", lambda en: en.scalar_tensor_tensor(out=ym[:], in0=ym[:], scalar=col(149, c), in1=t2[:],
                                                               op0=ALU.mult, op1=ALU.mult), [ym, t2, rwp], [ym])
                p.op("dve", lambda en: en.tensor_scalar(out=ym[:], in0=ym[:], scalar1=col(165, c), scalar2=None,
                                                        op0=ALU.add), [ym, rwp], [ym])
                p.op("dve", lambda en: en.scalar_tensor_tensor(out=t1[:], in0=rS[:], scalar=col(133, c), in1=kS[:],
                                                               op0=ALU.mult, op1=ALU.mult), [rS, kS, rwp], [t1])
                p.op("pe", lambda en: en.matmul(acc[:, 0:seg], C.bd64[:], t1[:], start=True, stop=True),
                     [C.bd64, t1], [acc])
                p.op("dve", lambda en: en.tensor_tensor(out=t2[:], in0=acc[:, 0:seg], in1=vS[:], op=ALU.mult),
                     [acc, vS], [t2])
                p.op("dve", lambda en: en.tensor_tensor(out=ym[:], in0=ym[:], in1=t2[:], op=ALU.add), [ym, t2], [ym])
                obb = ob[sg % 2]
                p.op("dve", lambda en: en.tensor_tensor(out=obb[:], in0=ym[:], in1=gT[:], op=ALU.mult), [ym, gT], [obb])
                p.dma("act", C.mT[cs, ss_], obb[:], reads=[obb], writes=[C.mT])


def mixer_odd(p, C, o, T, parts="rd"):
    if "r" in parts:
        rwkv_part(p, C, o, T)
    if "d" in parts:
        dsa_prep(p, C, o, T)
        dsa_attn(p, C, o, T)


def build(T, layers, stop_after=None, debug=False, vfirst_in=False, parts="rd"):
    p = Prog()
    C = Ctx()
    C.gpar = 0
    C.wi = 0
    C.used = []
    xin = p.dram("xT", [D, T], F32, "ExternalInput")
    out = p.dram("outT", [D, T], F32, "ExternalOutput")

    def wd(name, shape):
        C.used.append(name)
        return p.dram(name, shape, F32, "ExternalInput")

    gn_d = p.dram("gn", [128, 8, KC], F32, "ExternalInput")
    sm_d = p.dram("sm", [128, 8], F32, "ExternalInput")
    cw_d = p.dram("cw", [128, 96], F32, "ExternalInput")
    C.dalam = p.dram("dalam", [2, 256], F32, "ExternalInput")
    cf_d = p.dram("cf32", [128, 5 * 128 + 4], F32, "ExternalInput")
    has_odd = any(l % 2 == 1 for l in layers)
    NB = T // 128
    if has_odd:
        sm2_d = p.dram("sm2", [128, 6], F32, "ExternalInput")
        C.rope128c = p.dram("rope128c", [128, T], F32, "ExternalInput")
        C.rope128s = p.dram("rope128s", [128, T], F32, "ExternalInput")
        C.trid = p.dram("trid", [128, 128], F32, "ExternalInput")
        C.rwp = p.dram("rwp", [128, 2, NPRM], F32, "ExternalInput")
        C.cmd = p.dram("cmd", [128, SEG], F32, "ExternalInput")
        C.m320d = p.dram("m320d", [64, 320], F32, "ExternalInput")
        C.lora = {}
        C.kdr = p.dram("kdr", [128, T], BF16, "Internal")
        C.kir = p.dram("kir", [64, T], BF16, "Internal")
        C.vtk = p.dram("vtk", [128, NB, 128], BF16, "Internal")
        C.wtk = p.dram("wtk", [128, NB, 16], F32, "Internal")
        C.qdr = p.dram("qdr", [2048, T], BF16, "Internal")
        C.qir = p.dram("qir", [1024, T], BF16, "Internal")
    C.rope64c = p.dram("rope64c", [128, T], F32, "ExternalInput")
    C.rope64s = p.dram("rope64s", [128, T], F32, "ExternalInput")
    C.dmask = p.dram("dmask", [128, 4, 512], BF16, "ExternalInput")
    dk = "ExternalOutput" if debug else "Internal"
    C.zT = p.dram("zT", [EV_IN, T], F32, dk)
    C.mT = p.dram("mT", [D, T], BF16, dk)
    C.vfirst = p.dram("vfirst", [2048, T], F32, "ExternalInput" if vfirst_in else "Internal")

    C.bank = [p.ps(name=f"bank{i}") for i in range(8)]
    C.ci = 0
    C.Wb_in = p.dram("Wb_in", [24, 128, KC, 512], BF16, "Internal")
    C.Wb_out = p.dram("Wb_out", [8, 128, KC, 512], BF16, "Internal")
    C.Wb_gu = p.dram("Wb_gu", [43, 128, KC, 512], BF16, "Internal")
    C.Wb_dn = p.dram("Wb_dn", [8, 128, FC, 512], BF16, "Internal")
    C.sqb = [p.sb([128, TT], BF16, f"sqb{i}") for i in range(2)]
    C.rstd = p.sb([128, TT], F32, "rstd")
    C.gn = p.sb([128, 8, KC], F32, "gn")
    C.sm = p.sb([128, 8], F32, "sm")
    C.cw = p.sb([128, 96], F32, "cw")
    C.ones_f = p.sb([128, 128], F32, "ones_f")
    C.bd64 = p.sb([128, 128], F32, "bd64")
    C.ident = p.sb([128, 128], F32, "ident")
    C.rm64 = p.sb([128, 128], F32, "rm64")
    C.rm128 = p.sb([128, 128], F32, "rm128")
    C.cst = p.sb([128, 4], F32, "cst")
    C.ones_bf = p.sb([128, 128], BF16, "ones_bf")
    if has_odd:
        C.sm2 = p.sb([128, 6], F32, "sm2")
        p.dma("sp", C.sm2[:], sm2_d[:], reads=[sm2_d], writes=[C.sm2])
    p.dma("sp", C.gn[:], gn_d[:], reads=[gn_d], writes=[C.gn])
    p.dma("sp", C.sm[:], sm_d[:], reads=[sm_d], writes=[C.sm])
    p.dma("sp", C.cw[:], cw_d[:], reads=[cw_d], writes=[C.cw])
    for i, b in enumerate((C.ones_f, C.bd64, C.ident, C.rm64, C.rm128)):
        p.dma("sp", b[:], cf_d[:, i * 128:(i + 1) * 128], reads=[cf_d], writes=[b])
    p.dma("sp", C.cst[:], cf_d[:, 640:644], reads=[cf_d], writes=[C.cst])
    p.op("dve", lambda e: e.tensor_copy(C.ones_bf[:], C.ones_f[:]), [C.ones_f], [C.ones_bf])
    C.ident_bf = p.sb([128, 128], BF16, "ident_bf")
    p.op("dve", lambda e: e.tensor_copy(C.ident_bf[:], C.ident[:]), [C.ident], [C.ident_bf])

    for li, l in enumerate(layers):
        src = xin if li == 0 else out
        nin = EV_IN if l % 2 == 0 else OD_IN
        phase_in(p, C, l, src, wd(f"win{l}", [D, nin]), nin, T)
        if stop_after == ("in", l):
            break
        if l % 2 == 0:
            mixer_even(p, C, l // 2, T, save_vfirst=(l == 0))
        else:
            o = l // 2
            C.lora[o] = (wd(f"w2_{o}", [96, 2048]), wd(f"a2_{o}", [96, 2048]), wd(f"v2_{o}", [64, 2048]),
                         wd(f"g2_{o}", [256, 2048]))
            mixer_odd(p, C, o, T, parts)
        if stop_after == ("mix", l):
            break
        phase_out_ffn(p, C, l, src, out, wd(f"wout{l}", [D, D]), wd(f"gate{l}", [D, FH]), wd(f"up{l}", [D, FH]),
                      wd(f"down{l}", [FH, D]), T)
    p.barrier()
    p.es.close()
    return p, C


def rope_tables(T, rows, half):
    inv = (10000.0 ** (-np.arange(half, dtype=np.float32) / half)).astype(np.float32)
    ang = np.arange(T, dtype=np.float32)[None, :] * inv[:, None]
    idx = np.arange(rows) % half
    return np.cos(ang)[idx].astype(np.float32), np.sin(ang)[idx].astype(np.float32)


def rot_matrix(gsz):
    m = np.zeros((128, 128), np.float32)
    half = gsz // 2
    for g0 in range(0, 128, gsz):
        for d in range(half):
            m[g0 + d + half, g0 + d] = -1.0
            m[g0 + d, g0 + d + half] = 1.0
    return m


def host_consts(T):
    cf = np.zeros((128, 5 * 128 + 4), np.float32)
    cf[:, 0:128] = 1.0
    bd = np.zeros((128, 128), np.float32)
    bd[0:64, 0:64] = 1.0
    bd[64:128, 64:128] = 1.0
    cf[:, 128:256] = bd
    cf[:, 256:384] = np.eye(128, dtype=np.float32)
    cf[:, 384:512] = rot_matrix(64)
    cf[:, 512:640] = rot_matrix(128)
    cf[:, 640] = EPS
    cf[:, 641] = -1.0e29
    cf[:, 642] = 64e-5
    cf[:, 643] = 1.0e-24
    c64, s64 = rope_tables(T, 128, 32)
    c128, s128 = rope_tables(T, 128, 64)
    ii = np.arange(128)
    trid = np.where(ii[None, :] <= ii[:, None], 0.0, NEG).astype(np.float32)
    cmd = np.ones((128, SEG), np.float32)
    cmd[:, ::64] = 0.0
    i64 = np.arange(64)
    su = (i64[:, None] < i64[None, :]).astype(np.float32)
    iu = (i64[:, None] <= i64[None, :]).astype(np.float32)
    sl = (i64[:, None] > i64[None, :]).astype(np.float32)
    m320 = np.concatenate([su, iu, su, iu, sl], axis=1)
    s_ = np.arange(128)[:, None, None]
    jj = np.arange(4)[None, :, None]
    q_ = np.arange(512)[None, None, :]
    dmask = ((jj * 128 + s_) <= q_).astype(np.float32).astype(ml_dtypes.bfloat16)
    return dict(cf32=cf, rope64c=c64, rope64s=s64, dmask=dmask, rope128c=c128, rope128s=s128, trid=trid, cmd=cmd,
                m320d=m320)


def host_params(inp):
    gn = np.zeros((128, 8, KC), np.float32)
    for l in range(4):
        gn[:, l, :] = inp["mix_norm"][l].reshape(KC, 128).T
        gn[:, 4 + l, :] = inp["ffn_norm"][l].reshape(KC, 128).T
    sm = np.zeros((128, 8), np.float32)
    for e in range(2):
        sm[:, e] = np.tile(inp["da_q_norm"][e], 2)
        sm[:, 2 + e] = np.tile(inp["da_k_norm"][e], 2)
        sm[:, 4 + e] = inp["da_subln"][e]
    cw = np.zeros((128, 2, 3, 16), np.float32)
    for e in range(2):
        for j in range(3):
            cw[:, e, j, :] = inp["sc_conv"][e, j].reshape(16, 128).T
    out = dict(gn=gn, sm=sm, cw=cw.reshape(128, 96),
               dalam=np.ascontiguousarray(inp["da_lambda"].reshape(2, 256)))
    if "rw_mu" in inp:
        sm2 = np.zeros((128, 6), np.float32)
        rwp = np.zeros((128, 2, NPRM), np.float32)
        for o in range(2):
            sm2[:, o] = inp["sa_q_norm"][o]
            sm2[:, 2 + o] = inp["sa_k_norm"][o]
            sm2[:, 4 + o] = np.tile(inp["idx_k_norm"][o], 2)
            mu = inp["rw_mu"][o]
            rwp[:, o, 0:48] = mu[0:6144].reshape(48, 128).T
            rwp[0:96, o, 48] = mu[6144:6240]
            rwp[0:96, o, 49] = mu[6240:6336]
            rwp[0:64, o, 50] = mu[6336:6400]
            rwp[:, o, 51:53] = mu[6400:6656].reshape(2, 128).T
            for base, key in ((53, "rw_w0"), (69, "rw_a0"), (85, "rw_v0"), (101, "rw_k_k"), (117, "rw_k_a"),
                              (133, "rw_r_k"), (149, "rw_lnx_w"), (165, "rw_lnx_b")):
                rwp[:, o, base:base + 16] = inp[key][o].reshape(16, 128).T
        out["sm2"] = sm2
        out["rwp"] = rwp
    return out


def weight_of(inp, name):
    kind, l = name.rstrip("0123456789"), int(name[-1])
    if kind == "gate":
        return inp["ffn_gate"][l]
    if kind == "up":
        return inp["ffn_up"][l]
    if kind == "down":
        return inp["ffn_down"][l]
    if kind == "win":
        return inp["ev_w_in"][l // 2] if l % 2 == 0 else inp["od_w_in"][l // 2]
    if kind == "wout":
        return inp["ev_w_out"][l // 2] if l % 2 == 0 else inp["od_w_out"][l // 2]
    if kind == "w2_":
        return inp["rw_w2"][l]
    if kind == "a2_":
        return inp["rw_a2"][l]
    if kind == "v2_":
        return inp["rw_v2"][l]
    if kind == "g2_":
        return inp["rw_g2"][l]
    raise KeyError(name)


def make_in_maps(inp, T, used, ncores, names=None):
    cs = host_consts(T)
    ps = host_params(inp)
    maps = []
    for b in range(ncores):
        m = dict(cs)
        m.update(ps)
        m = {k: v for k, v in m.items() if k in names}
        m["xT"] = np.ascontiguousarray(inp["x"][b, :T].T)
        for name in used:
            m[name] = weight_of(inp, name)
        maps.append(m)
    return maps


def kernel(**inp):
    inp = {k: np.asarray(v) for k, v in inp.items()}
    T = 4096
    layers = [0, 1, 2, 3]
    p, C = build(T, layers)
    maps = make_in_maps(inp, T, C.used, 2, set(p.in_names))
    res = run_bass_kernel_spmd(p.nc, maps, core_ids=[0, 1])
    outp = np.stack([res.results[b]["outT"].T for b in range(2)], axis=0)
    return np.ascontiguousarray(outp.astype(np.float32))
```

```python
import contextlib
import math
import numpy as np
import ml_dtypes
import concourse.bass as bass
import concourse.mybir as mybir
from concourse.bass_utils import run_bass_kernel_spmd

F32 = mybir.dt.float32
BF16 = mybir.dt.bfloat16
AF = mybir.ActivationFunctionType
ALU = mybir.AluOpType
AX = mybir.AxisListType

NDS = 8
D = 4096
KC = 32
TT = 512
FH = 11008
FC = 86
EV_IN = 12288
OD_IN = 10064
EPS = 1e-6


class Buf:
    def __init__(self, t, name):
        self.t = t
        self.name = name
        self.lw = None
        self.rd = {}

    def __getitem__(self, k):
        return self.t[k]


class Prog:
    def __init__(self):
        nc = self.nc = bass.Bass("TRN2", target_bir_lowering=False)
        self.es = contextlib.ExitStack()
        self.eng = dict(pe=nc.tensor, act=nc.scalar, dve=nc.vector, pool=nc.gpsimd, sp=nc.sync)
        self.semh = {}
        self.cnt = {}
        for e in self.eng:
            self.semh[e] = self.es.enter_context(nc.semaphore(f"s_{e}"))
            self.cnt[e] = 0
        self.known = {e: {} for e in self.eng}
        self.dq = {}
        for q in ("sp", "act", "pool"):
            ks = []
            for i in range(NDS):
                k = f"d_{q}{i}"
                self.semh[k] = self.es.enter_context(nc.semaphore(k))
                self.cnt[k] = 0
                ks.append(k)
            self.dq[q] = [ks, 0]
        self.nbuf = 0
        self.in_names = []
        self.cur = self.es

    def sb(self, shape, dt, name=None):
        self.nbuf += 1
        name = f"S{self.nbuf}_" + (name or "sb")
        t = self.cur.enter_context(self.nc.sbuf_tensor(name, list(shape), dt))
        return Buf(t, name)

    def ps(self, shape=(128, 512), dt=F32, name=None):
        self.nbuf += 1
        name = name or f"ps{self.nbuf}"
        t = self.es.enter_context(self.nc.psum_tensor(name, list(shape), dt))
        return Buf(t, name)

    def dram(self, name, shape, dt, kind):
        if kind == "ExternalInput":
            self.in_names.append(name)
        t = self.nc.dram_tensor(name, list(shape), dt, kind=kind)
        return Buf(t.ap(), name)

    @contextlib.contextmanager
    def phase(self):
        old = self.cur
        with contextlib.ExitStack() as st:
            self.cur = st
            yield st
            self.barrier()
        self.cur = old

    def barrier(self):
        evs = [(k, v) for k, v in self.cnt.items() if v > 0]
        for e in self.eng:
            self._wait(e, evs)

    def _wait(self, e, evs):
        kn = self.known[e]
        for (k, v) in evs:
            if kn.get(k, 0) >= v:
                continue
            self.eng[e].wait_ge(self.semh[k], v)
            kn[k] = v

    def _deps(self, reads, writes):
        evs = []
        for b in reads:
            if b.lw:
                evs.append(b.lw)
        for b in writes:
            if b.lw:
                evs.append(b.lw)
            evs.extend(b.rd.items())
        return evs

    def _record(self, ev, reads, writes):
        for b in reads:
            b.rd[ev[0]] = ev[1]
        for b in writes:
            b.lw = ev
            b.rd = {}

    def op(self, e, fn, reads=(), writes=()):
        evs = self._deps(reads, writes)
        if e == "pe":
            evs = [ev for ev in evs if ev[0] != "pe"]
        self._wait(e, evs)
        ins = fn(self.eng[e])
        self.cnt[e] += 1
        ins.then_inc(self.semh[e], 1)
        ev = (e, self.cnt[e])
        self._record(ev, reads, writes)
        return ev

    def dma(self, q, out, in_, reads=(), writes=(), **kw):
        ks, n = self.dq[q]
        k = ks[n % NDS]
        self.dq[q][1] = n + 1
        evs = self._deps(reads, writes)
        if self.cnt[k] > 0:
            evs.append((k, self.cnt[k]))
        self._wait(q, evs)
        ins = self.eng[q].dma_start(out=out, in_=in_, **kw)
        self.cnt[k] += 16
        ins.then_inc(self.semh[k], 16)
        ev = (k, self.cnt[k])
        self._record(ev, reads, writes)
        return ev


class Ctx:
    pass


KB = 8


def group_blocks(segs):
    blocks = []
    off = 0
    for si, (wb_, ap, c0, ncol) in enumerate(segs):
        o = 0
        while o < ncol:
            m = min(128, ncol - o)
            blocks.append((si, off + o, m, c0 + o))
            o += m
        off += ncol
    return blocks, off


def cast_weights(p, C, groups, nk, Wb):
    for gi, segs in enumerate(groups):
        blocks, tot = group_blocks(segs)
        for k0 in range(0, nk, KB):
            kb = min(KB, nk - k0)
            st = C.cst_f[C.ci % 2]
            bf = C.cst_b[C.ci % 2]
            off = 0
            for (wb_, ap, c0, ncol) in segs:
                p.dma("sp", st[:, 0:kb, off:off + ncol],
                      ap[k0 * 128:(k0 + kb) * 128, c0:c0 + ncol].rearrange("(c p) n -> p c n", p=128),
                      reads=[wb_], writes=[st])
                off += ncol
            if C.ci % 2 == 0:
                p.op("dve", lambda e: e.tensor_copy(bf[:, 0:kb, 0:tot], st[:, 0:kb, 0:tot]), [st], [bf])
            else:
                p.op("act", lambda e: e.copy(bf[:, 0:kb, 0:tot], st[:, 0:kb, 0:tot]), [st], [bf])
            C.ci += 1
            p.dma("pool", Wb[gi, :, k0:k0 + kb, 0:tot], bf[:, 0:kb, 0:tot], reads=[bf], writes=[Wb])


def linear(p, C, hT, nk, groups, Wb, epi):
    for gi, segs in enumerate(groups):
        banks = C.bank[(C.gpar % 2) * 4:(C.gpar % 2) * 4 + 4]
        C.gpar += 1
        blocks, tot = group_blocks(segs)
        for k0 in range(0, nk, C.wkb):
            kb = min(C.wkb, nk - k0)
            wt = C.wbig[C.wi % len(C.wbig)]
            C.wi += 1
            p.dma("sp", wt[:, 0:kb, 0:tot], Wb[gi, :, k0:k0 + kb, 0:tot], reads=[Wb], writes=[wt])
            for kk in range(kb):
                ki = k0 + kk
                for bi, (si, o, m, cc) in enumerate(blocks):
                    p.op("pe", lambda e: e.matmul(banks[bi][0:m, :], wt[:, kk, o:o + m], hT[:, ki, :],
                                                  start=(ki == 0), stop=(ki == nk - 1)), [wt, hT], [banks[bi]])
        epi(gi, blocks, banks)


def col_groups(wbuf, ap, n, gsz=512):
    return [[(wbuf, ap, c0, min(gsz, n - c0))] for c0 in range(0, n, gsz)]


def rms_tile(p, C, xt, hT, gcol):
    acc = C.bank[7]
    for c in range(KC):
        sq = C.sqb[c % 2]
        p.op("act", lambda e: e.activation(out=sq[:], in_=xt[:, c, :], func=AF.Square), [xt], [sq])
        p.op("pe", lambda e: e.matmul(acc[:], C.ones_bf[:], sq[:], start=(c == 0), stop=(c == KC - 1)),
             [C.ones_bf, sq], [acc])
    p.op("act", lambda e: e.activation(out=C.rstd[:], in_=acc[:], func=AF.Sqrt, scale=1.0 / D,
                                       bias=C.cst[:, 0:1]), [acc, C.cst], [C.rstd])
    p.op("dve", lambda e: e.reciprocal(C.rstd[:], C.rstd[:]), [C.rstd], [C.rstd])
    for c in range(KC):
        p.op("dve", lambda e: e.scalar_tensor_tensor(out=hT[:, c, :], in0=xt[:, c, :], scalar=gcol(c),
                                                     in1=C.rstd[:], op0=ALU.mult, op1=ALU.mult),
             [xt, C.rstd, C.gn], [hT])


def load_tile_fm(p, q, dst, src, t0, nchunks, step=8):
    for c0 in range(0, nchunks, step):
        p.dma(q, dst[:, c0:c0 + step, :],
              src[c0 * 128:(c0 + step) * 128, t0:t0 + TT].rearrange("(c p) t -> p c t", p=128),
              reads=[src], writes=[dst])


def store_tile_fm(p, q, dst, src, t0, nchunks, step=8):
    for c0 in range(0, nchunks, step):
        p.dma(q, dst[c0 * 128:(c0 + step) * 128, t0:t0 + TT].rearrange("(c p) t -> p c t", p=128),
              src[:, c0:c0 + step, :], reads=[src], writes=[dst])


def phase_in(p, C, l, resid, Win, nin, T):
    groups = col_groups(Win, Win.t, nin)
    with p.phase():
        C.cst_f = [p.sb([128, KB, 512], F32, f"cf{i}") for i in range(2)]
        C.cst_b = [p.sb([128, KB, 512], BF16, f"cb{i}") for i in range(2)]
        cast_weights(p, C, groups, KC, C.Wb_in)
    with p.phase():
        C.wkb = 8
        C.wbig = [p.sb([128, 8, 512], BF16, f"wbig{i}") for i in range(3)]
        xt = p.sb([128, KC, TT], F32, "A_xt")
        hTs = [p.sb([128, KC, TT], BF16, f"A_hT{i}") for i in range(2)]
        zst = [p.sb([128, TT], F32, f"A_z{i}") for i in range(4)]
        zi = [0]
        for tt in range(T // TT):
            t0 = tt * TT
            hT = hTs[tt % 2]
            load_tile_fm(p, "act", xt, resid, t0, KC)
            rms_tile(p, C, xt, hT, lambda c: C.gn[:, l, c:c + 1])

            def epi(gi, blocks, banks):
                for bi, (si, o, m, cc) in enumerate(blocks):
                    z = zst[zi[0] % 4]
                    zi[0] += 1
                    if bi % 2 == 0:
                        p.op("act", lambda e: e.copy(z[0:m, :], banks[bi][0:m, :]), [banks[bi]], [z])
                    else:
                        p.op("dve", lambda e: e.tensor_copy(z[0:m, :], banks[bi][0:m, :]), [banks[bi]], [z])
                    p.dma("act", C.zT[cc:cc + m, t0:t0 + TT], z[0:m, :], reads=[z], writes=[C.zT])

            linear(p, C, hT, KC, groups, C.Wb_in, epi)


def phase_out_ffn(p, C, l, src, dst, Wout, Wg, Wu, Wd, T):
    g_out = col_groups(Wout, Wout.t, D)
    ggroups = [[(Wg, Wg.t, c0, 256), (Wu, Wu.t, c0, 256)] for c0 in range(0, FH, 256)]
    g_dn = col_groups(Wd, Wd.t, D)
    with p.phase():
        C.cst_f = [p.sb([128, KB, 512], F32, f"cf{i}") for i in range(2)]
        C.cst_b = [p.sb([128, KB, 512], BF16, f"cb{i}") for i in range(2)]
        cast_weights(p, C, g_out, KC, C.Wb_out)
        cast_weights(p, C, ggroups, KC, C.Wb_gu)
        cast_weights(p, C, g_dn, FC, C.Wb_dn)
    with p.phase():
        C.wkb = 4
        C.wbig = [p.sb([128, 4, 512], BF16, f"wbig{i}") for i in range(3)]
        xt = p.sb([128, KC, TT], F32, "C_xt")
        mh = p.sb([128, KC, TT], BF16, "C_mh")
        act = p.sb([128, FC, TT], BF16, "C_act")
        tmp = [p.sb([128, TT], F32, f"C_tmp{i}") for i in range(2)]
        ti = [0]
        for tt in range(T // TT):
            t0 = tt * TT
            load_tile_fm(p, "act", xt, src, t0, KC)
            load_tile_fm(p, "act", mh, C.mT, t0, KC)

            def epi_add(gi, blocks, banks):
                for bi, (si, o, m, cc) in enumerate(blocks):
                    c = cc // 128
                    p.op("dve", lambda e: e.tensor_tensor(out=xt[:, c, :], in0=xt[:, c, :], in1=banks[bi][:],
                                                          op=ALU.add), [xt, banks[bi]], [xt])

            linear(p, C, mh, KC, g_out, C.Wb_out, epi_add)
            rms_tile(p, C, xt, mh, lambda c: C.gn[:, 4 + l, c:c + 1])

            def epi_glu(gi, blocks, banks):
                nb = len(blocks) // 2
                for bi in range(nb):
                    cc = blocks[bi][3]
                    c = cc // 128
                    t_ = tmp[ti[0] % 2]
                    ti[0] += 1
                    p.op("act", lambda e: e.activation(out=t_[:], in_=banks[bi][:], func=AF.Silu), [banks[bi]], [t_])
                    p.op("dve", lambda e: e.tensor_tensor(out=act[:, c, :], in0=t_[:], in1=banks[nb + bi][:],
                                                          op=ALU.mult), [t_, banks[nb + bi]], [act])

            linear(p, C, mh, KC, ggroups, C.Wb_gu, epi_glu)
            linear(p, C, act, FC, g_dn, C.Wb_dn, epi_add)
            store_tile_fm(p, "act", dst, xt, t0, KC)


def norm_rope(p, C, src, dst, gcol, bd, rm, cos, sin, rs, tA, tB, T, gsz, do_norm=True):
    acc = C.bank[7]
    nsl = T // 512
    if do_norm:
        for j in range(nsl):
            sl = slice(j * 512, (j + 1) * 512)
            sq = C.sq32[j % 2]
            p.op("act", lambda e: e.activation(out=sq[:], in_=src[:, sl], func=AF.Square), [src], [sq])
            p.op("pe", lambda e: e.matmul(acc[:], bd[:], sq[:], start=True, stop=True), [bd, sq], [acc])
            p.op("act", lambda e: e.activation(out=rs[:, sl], in_=acc[:], func=AF.Sqrt, scale=1.0 / gsz,
                                               bias=C.cst[:, 0:1]), [acc, C.cst], [rs])
        p.op("dve", lambda e: e.reciprocal(rs[:, 0:T], rs[:, 0:T]), [rs], [rs])
        p.op("dve", lambda e: e.scalar_tensor_tensor(out=tA[:, 0:T], in0=src[:, 0:T], scalar=gcol, in1=rs[:, 0:T],
                                                     op0=ALU.mult, op1=ALU.mult), [src, rs, C.sm], [tA])
        cur = tA
    else:
        cur = src
    for j in range(nsl):
        sl = slice(j * 512, (j + 1) * 512)
        p.op("pe", lambda e: e.matmul(acc[:], rm[:], cur[:, sl], start=True, stop=True), [rm, cur], [acc])
        p.op("dve", lambda e: e.tensor_tensor(out=tB[:, sl], in0=acc[:], in1=sin[:, sl], op=ALU.mult),
             [acc, sin], [tB])
    p.op("dve", lambda e: e.tensor_tensor(out=tA[:, 0:T], in0=cur[:, 0:T], in1=cos[:, 0:T], op=ALU.mult),
         [cur, cos], [tA])
    p.op("dve", lambda e: e.tensor_tensor(out=dst[:, 0:T], in0=tA[:, 0:T], in1=tB[:, 0:T], op=ALU.add),
         [tA, tB], [dst])


def mixer_even(p, C, e, T, save_vfirst):
    lam_init = 0.8 - 0.6 * math.exp(-0.3 * (2 * e))
    with p.phase():
        C.sq32 = [p.sb([128, 512], F32, f"sq32{i}") for i in range(2)]
        cos = p.sb([128, T], F32, "E_cos")
        sin = p.sb([128, T], F32, "E_sin")
        qt = p.sb([128, T], F32, "E_qt")
        kt = p.sb([128, T], F32, "E_kt")
        vT = p.sb([128, T], F32, "E_vT")
        tA = p.sb([128, T], F32, "E_tA")
        tB = p.sb([128, T], F32, "E_tB")
        rs = p.sb([128, T], F32, "E_rs")
        qr = p.sb([128, T], BF16, "E_qr")
        kr = p.sb([128, T], BF16, "E_kr")
        vtok = p.sb([128, T // 128, 128], BF16, "E_vtok")
        pts = [p.sb([128, 512], BF16, f"E_pt{i}") for i in range(3)]
        msk = p.sb([128, 4, 512], BF16, "E_msk")
        o0 = p.sb([128, 512], F32, "E_o0")
        o1 = p.sb([128, 512], F32, "E_o1")
        rl = p.sb([128, 512], F32, "E_rl")
        ob = [p.sb([128, 512], BF16, f"E_ob{i}") for i in range(2)]
        lam = p.sb([128, 2], F32, "E_lam")
        lt = p.sb([1, 256], F32, "E_lt")
        lt2 = p.sb([1, 128], F32, "E_lt2")
        ls = p.sb([1, 2], F32, "E_ls")
        p.dma("act", cos[:], C.rope64c.t[:, 0:T], reads=[C.rope64c], writes=[cos])
        p.dma("act", sin[:], C.rope64s.t[:, 0:T], reads=[C.rope64s], writes=[sin])
        p.dma("act", msk[:], C.dmask.t[:, :, :], reads=[C.dmask], writes=[msk])
        if save_vfirst:
            for i in range(4):
                p.dma("act", C.vfirst[i * 512:(i + 1) * 512, :], C.zT[4096 + i * 512:4096 + (i + 1) * 512, 0:T],
                      reads=[C.zT], writes=[C.vfirst])
        p.dma("act", lt[:], C.dalam.t[e:e + 1, :], reads=[C.dalam], writes=[lt])
        p.op("dve", lambda en: en.tensor_tensor(out=lt2[:, 0:64], in0=lt[:, 0:64], in1=lt[:, 64:128], op=ALU.mult),
             [lt], [lt2])
        p.op("dve", lambda en: en.tensor_tensor(out=lt2[:, 64:128], in0=lt[:, 128:192], in1=lt[:, 192:256],
                                                op=ALU.mult), [lt], [lt2])
        p.op("dve", lambda en: en.reduce_sum(out=ls[:], in_=lt2[:].rearrange("a (b c) -> a b c", b=2), axis=AX.X),
             [lt2], [ls])
        p.op("act", lambda en: en.activation(out=ls[:], in_=ls[:], func=AF.Exp), [ls], [ls])
        p.op("dve", lambda en: en.scalar_tensor_tensor(out=ls[:, 0:1], in0=ls[:, 1:2], scalar=-lam_init,
                                                       in1=ls[:, 0:1], op0=ALU.add, op1=ALU.subtract), [ls], [ls])
        acc = C.bank[7]
        p.op("pe", lambda en: en.matmul(acc[:, 0:1], C.ones_f[0:1, :], ls[0:1, 0:1], start=True, stop=True),
             [C.ones_f, ls], [acc])
        p.op("dve", lambda en: en.tensor_copy(lam[:, 0:1], acc[:, 0:1]), [acc], [lam])
        p.op("dve", lambda en: en.tensor_scalar(out=lam[:, 1:2], in0=C.sm[:, 4 + e:5 + e], scalar1=1.0 - lam_init,
                                                scalar2=None, op0=ALU.mult), [C.sm], [lam])
        nqg = T // 512
        pti = 0
        for h in range(16):
            p.dma("act", qt[:], C.zT[h * 128:(h + 1) * 128, 0:T], reads=[C.zT], writes=[qt])
            p.dma("act", kt[:], C.zT[2048 + h * 128:2048 + (h + 1) * 128, 0:T], reads=[C.zT], writes=[kt])
            p.dma("act", vT[:], C.zT[4096 + h * 128:4096 + (h + 1) * 128, 0:T], reads=[C.zT], writes=[vT])
            norm_rope(p, C, qt, qr, C.sm[:, e:e + 1], C.bd64, C.rm64, cos, sin, rs, tA, tB, T, 64)
            norm_rope(p, C, kt, kr, C.sm[:, 2 + e:3 + e], C.bd64, C.rm64, cos, sin, rs, tA, tB, T, 64)
            for j4 in range(T // 512):
                for jj in range(4):
                    j = j4 * 4 + jj
                    p.op("pe", lambda en: en.transpose(acc[:, jj * 128:(jj + 1) * 128], vT[:, j * 128:(j + 1) * 128],
                                                       C.ident[:]), [vT, C.ident], [acc])
                p.op("act", lambda en: en.copy(vtok[:, j4 * 4:j4 * 4 + 4, :],
                                               acc[:].rearrange("p (a b) -> p a b", a=4)), [acc], [vtok])
            for g in range(nqg):
                qs = slice(g * 512, (g + 1) * 512)
                for c in range(2):
                    Oc = C.bank[3 + 2 * c]
                    Lc = C.bank[4 + 2 * c]
                    ps_ = slice(c * 64, (c + 1) * 64)
                    nj = 4 * g + 4
                    for j in range(nj):
                        sb_ = C.bank[pti % 3]
                        pt = pts[pti % 3]
                        pti += 1
                        p.op("pe", lambda en: en.matmul(sb_[:], kr[ps_, j * 128:(j + 1) * 128], qr[ps_, qs],
                                                        start=True, stop=True), [kr, qr], [sb_])
                        p.op("act", lambda en: en.activation(out=pt[:], in_=sb_[:], func=AF.Exp, scale=0.125),
                             [sb_], [pt])
                        if j >= 4 * g:
                            p.op("dve", lambda en: en.tensor_tensor(out=pt[:], in0=pt[:], in1=msk[:, j - 4 * g, :],
                                                                    op=ALU.mult), [pt, msk], [pt])
                        p.op("pe", lambda en: en.matmul(Oc[:], vtok[:, j, :], pt[:], start=(j == 0),
                                                        stop=(j == nj - 1)), [vtok, pt], [Oc])
                        p.op("pe", lambda en: en.matmul(Lc[:], C.ones_bf[:], pt[:], start=(j == 0),
                                                        stop=(j == nj - 1)), [C.ones_bf, pt], [Lc])
                p.op("dve", lambda en: en.reciprocal(rl[:], C.bank[4][:]), [C.bank[4]], [rl])
                p.op("dve", lambda en: en.tensor_tensor(out=o0[:], in0=C.bank[3][:], in1=rl[:], op=ALU.mult),
                     [C.bank[3], rl], [o0])
                p.op("dve", lambda en: en.reciprocal(rl[:], C.bank[6][:]), [C.bank[6]], [rl])
                p.op("dve", lambda en: en.tensor_tensor(out=o1[:], in0=C.bank[5][:], in1=rl[:], op=ALU.mult),
                     [C.bank[5], rl], [o1])
                p.op("dve", lambda en: en.scalar_tensor_tensor(out=o0[:], in0=o1[:], scalar=lam[:, 0:1], in1=o0[:],
                                                               op0=ALU.mult, op1=ALU.add), [o1, o0, lam], [o0])
                sq = C.sq32[0]
                p.op("act", lambda en: en.activation(out=sq[:], in_=o0[:], func=AF.Square), [o0], [sq])
                p.op("pe", lambda en: en.matmul(acc[:], C.ones_f[:], sq[:], start=True, stop=True),
                     [C.ones_f, sq], [acc])
                p.op("act", lambda en: en.activation(out=rl[:], in_=acc[:], func=AF.Sqrt, scale=1.0 / 128,
                                                     bias=C.cst[:, 0:1]), [acc, C.cst], [rl])
                p.op("dve", lambda en: en.reciprocal(rl[:], rl[:]), [rl], [rl])
                obb = ob[g % 2]
                p.op("dve", lambda en: en.scalar_tensor_tensor(out=obb[:], in0=o0[:], scalar=lam[:, 1:2], in1=rl[:],
                                                               op0=ALU.mult, op1=ALU.mult), [o0, rl, lam], [obb])
                p.dma("act", C.mT[h * 128:(h + 1) * 128, qs], obb[:], reads=[obb], writes=[C.mT])
        for c in range(16):
            p.dma("act", qt[:], C.zT[6144 + c * 128:6144 + (c + 1) * 128, 0:T], reads=[C.zT], writes=[qt])
            p.dma("act", kt[:], C.zT[8192 + c * 128:8192 + (c + 1) * 128, 0:T], reads=[C.zT], writes=[kt])
            p.dma("act", vT[:], C.zT[10240 + c * 128:10240 + (c + 1) * 128, 0:T], reads=[C.zT], writes=[vT])
            w = lambda j: C.cw[:, (e * 3 + j) * 16 + c:(e * 3 + j) * 16 + c + 1]
            p.op("dve", lambda en: en.tensor_tensor(out=tA[:], in0=kt[:], in1=vT[:], op=ALU.mult), [kt, vT], [tA])
            p.op("dve", lambda en: en.tensor_scalar(out=tB[:], in0=tA[:], scalar1=w(2), scalar2=None, op0=ALU.mult),
                 [tA, C.cw], [tB])
            p.op("dve", lambda en: en.scalar_tensor_tensor(out=tB[:, 1:T], in0=tA[:, 0:T - 1], scalar=w(1),
                                                           in1=tB[:, 1:T], op0=ALU.mult, op1=ALU.add),
                 [tA, tB, C.cw], [tB])
            p.op("dve", lambda en: en.scalar_tensor_tensor(out=tB[:, 2:T], in0=tA[:, 0:T - 2], scalar=w(0),
                                                           in1=tB[:, 2:T], op0=ALU.mult, op1=ALU.add),
                 [tA, tB, C.cw], [tB])
            p.op("dve", lambda en: en.tensor_tensor(out=qr[:], in0=qt[:], in1=tB[:], op=ALU.mult), [qt, tB], [qr])
            p.dma("act", C.mT[2048 + c * 128:2048 + (c + 1) * 128, 0:T], qr[:], reads=[qr], writes=[C.mT])


SA_SCALE = 128 ** -0.5
NEG = -1.0e30


def dsa_prep(p, C, o, T):
    NB = T // 128
    with p.phase():
        C.sq32 = [p.sb([128, 512], F32, f"sq32{i}") for i in range(2)]
        c128 = p.sb([128, T], F32, "P_c128")
        s128 = p.sb([128, T], F32, "P_s128")
        c64 = p.sb([128, T], F32, "P_c64")
        s64 = p.sb([128, T], F32, "P_s64")
        qt = p.sb([128, T], F32, "P_qt")
        tA = p.sb([128, T], F32, "P_tA")
        tB = p.sb([128, T], F32, "P_tB")
        rs = p.sb([128, T], F32, "P_rs")
        ob = p.sb([128, T], BF16, "P_ob")
        vtok = p.sb([128, NB, 128], BF16, "P_vtok")
        wit = p.sb([128, NB, 16], F32, "P_wit")
        acc = C.bank[7]
        p.dma("act", c128[:], C.rope128c.t[:, 0:T], reads=[C.rope128c], writes=[c128])
        p.dma("act", s128[:], C.rope128s.t[:, 0:T], reads=[C.rope128s], writes=[s128])
        p.dma("act", c64[:], C.rope64c.t[:, 0:T], reads=[C.rope64c], writes=[c64])
        p.dma("act", s64[:], C.rope64s.t[:, 0:T], reads=[C.rope64s], writes=[s64])
        p.dma("act", qt[:], C.zT[8704:8832, 0:T], reads=[C.zT], writes=[qt])
        norm_rope(p, C, qt, ob, C.sm2[:, 2 + o:3 + o], C.ones_f, C.rm128, c128, s128, rs, tA, tB, T, 128)
        p.dma("act", C.kdr[:, 0:T], ob[:], reads=[ob], writes=[C.kdr])
        p.dma("act", qt[0:64, :], C.zT[9984:10048, 0:T], reads=[C.zT], writes=[qt])
        p.dma("act", qt[64:128, :], C.zT[9984:10048, 0:T], reads=[C.zT], writes=[qt])
        norm_rope(p, C, qt, ob, C.sm2[:, 4 + o:5 + o], C.bd64, C.rm64, c64, s64, rs, tA, tB, T, 64)
        p.dma("act", C.kir[:, 0:T], ob[0:64, :], reads=[ob], writes=[C.kir])
        p.dma("act", qt[:], C.zT[8832:8960, 0:T], reads=[C.zT], writes=[qt])
        for j4 in range(T // 512):
            for jj in range(4):
                j = j4 * 4 + jj
                p.op("pe", lambda en: en.transpose(acc[:, jj * 128:(jj + 1) * 128], qt[:, j * 128:(j + 1) * 128],
                                                   C.ident[:]), [qt, C.ident], [acc])
            p.op("act", lambda en: en.copy(vtok[:, j4 * 4:j4 * 4 + 4, :], acc[:].rearrange("p (a b) -> p a b", a=4)),
                 [acc], [vtok])
        p.dma("act", C.vtk[:, 0:NB, :], vtok[:], reads=[vtok], writes=[C.vtk])
        p.dma("act", tA[0:16, :], C.zT[10048:10064, 0:T], reads=[C.zT], writes=[tA])
        for j in range(NB):
            p.op("pe", lambda en: en.transpose(acc[:, 0:16], tA[0:16, j * 128:(j + 1) * 128], C.ident[0:16, 0:16]),
                 [tA, C.ident], [acc])
            p.op("act", lambda en: en.mul(wit[:, j, :], acc[:, 0:16], 1.0 / 32.0), [acc], [wit])
        p.dma("act", C.wtk[:, 0:NB, :], wit[:], reads=[wit], writes=[C.wtk])
        for h in range(16):
            p.dma("act", qt[:], C.zT[6656 + h * 128:6656 + (h + 1) * 128, 0:T], reads=[C.zT], writes=[qt])
            norm_rope(p, C, qt, ob, C.sm2[:, o:o + 1], C.ones_f, C.rm128, c128, s128, rs, tA, tB, T, 128)
            p.dma("act", C.qdr[h * 128:(h + 1) * 128, 0:T], ob[:], reads=[ob], writes=[C.qdr])
        for c in range(8):
            p.dma("act", qt[:], C.zT[8960 + c * 128:8960 + (c + 1) * 128, 0:T], reads=[C.zT], writes=[qt])
            norm_rope(p, C, qt, ob, None, None, C.rm64, c64, s64, rs, tA, tB, T, 64, do_norm=False)
            p.dma("act", C.qir[c * 128:(c + 1) * 128, 0:T], ob[:], reads=[ob], writes=[C.qir])


def dsa_attn(p, C, o, T):
    NB = T // 128
    with p.phase():
        kdr = p.sb([128, T], BF16, "D_kdr")
        kir = p.sb([128, T], BF16, "D_kir")
        vtok = p.sb([128, NB, 128], BF16, "D_vtok")
        wit = p.sb([128, NB, 16], F32, "D_wit")
        tri = p.sb([128, 128], F32, "D_tri")
        Iacc = p.sb([128, T], F32, "D_I")
        work = p.sb([128, T], F32, "D_w")
        maskT = p.sb([128, NB, 128], BF16, "D_mT")
        m8 = p.sb([128, 8], F32, "D_m8")
        qd2 = [p.sb([128, 16, 128], BF16, f"D_qd{i}") for i in range(2)]
        qi2 = [p.sb([128, 16, 128], BF16, f"D_qi{i}") for i in range(2)]
        rl = [p.sb([128, 512], F32, f"D_rl{i}") for i in range(2)]
        pts = [p.sb([128, 512], BF16, f"D_pt{i}") for i in range(3)]
        osb = [p.sb([128, 512], BF16, f"D_os{i}") for i in range(2)]
        rcp = p.sb([128, 512], F32, "D_rcp")
        acc = C.bank[7]
        p.dma("act", kdr[:], C.kdr[:, 0:T], reads=[C.kdr], writes=[kdr])
        p.dma("act", kir[0:64, :], C.kir[:, 0:T], reads=[C.kir], writes=[kir])
        p.dma("act", vtok[:], C.vtk[:, 0:NB, :], reads=[C.vtk], writes=[vtok])
        p.dma("act", wit[:], C.wtk[:, 0:NB, :], reads=[C.wtk], writes=[wit])
        p.dma("act", tri[:], C.trid.t[:, :], reads=[C.trid], writes=[tri])
        bi = 0
        pti = 0
        for qb in range(NB):
            q0 = qb * 128
            nb = qb + 1
            L = nb * 128
            qd = qd2[qb % 2]
            qi = qi2[qb % 2]
            p.dma("act", qd[:], C.qdr[:, q0:q0 + 128].rearrange("(h d) q -> d h q", d=128), reads=[C.qdr], writes=[qd])
            p.dma("act", qi[0:64, :, :], C.qir[:, q0:q0 + 128].rearrange("(h d) q -> d h q", d=64),
                  reads=[C.qir], writes=[qi])
            for s0 in range(0, L, 512):
                w = min(512, L - s0)
                for h in range(16):
                    bk = C.bank[bi % 3]
                    r_ = rl[bi % 2]
                    bi += 1
                    p.op("pe", lambda en: en.matmul(bk[:, 0:w], qi[0:64, h, :], kir[0:64, s0:s0 + w], start=True,
                                                    stop=True), [qi, kir], [bk])
                    p.op("act", lambda en: en.activation(out=r_[:, 0:w], in_=bk[:, 0:w], func=AF.Relu), [bk], [r_])
                    if h == 0:
                        p.op("dve", lambda en: en.tensor_scalar(out=Iacc[:, s0:s0 + w], in0=r_[:, 0:w],
                                                                scalar1=wit[:, qb, 0:1], scalar2=None, op0=ALU.mult),
                             [r_, wit], [Iacc])
                    else:
                        p.op("dve", lambda en: en.scalar_tensor_tensor(out=Iacc[:, s0:s0 + w], in0=r_[:, 0:w],
                                                                       scalar=wit[:, qb, h:h + 1],
                                                                       in1=Iacc[:, s0:s0 + w], op0=ALU.mult,
                                                                       op1=ALU.add), [r_, wit, Iacc], [Iacc])
            p.op("dve", lambda en: en.tensor_tensor(out=Iacc[:, q0:q0 + 128], in0=Iacc[:, q0:q0 + 128], in1=tri[:],
                                                    op=ALU.add), [Iacc, tri], [Iacc])
            if qb >= 2:
                p.op("act", lambda en: en.copy(work[:, 0:L], Iacc[:, 0:L]), [Iacc], [work])
                for r in range(32):
                    p.op("dve", lambda en: en.max(out=m8[:], in_=work[:, 0:L]), [work], [m8])
                    if r < 31:
                        p.op("dve", lambda en: en.match_replace(out=work[:, 0:L], in_to_replace=m8[:],
                                                                in_values=work[:, 0:L], imm_value=-3.0e38),
                             [work, m8], [work])
                thr = m8[:, 7:8]
                thr_b = [m8]
            else:
                thr = C.cst[:, 1:2]
                thr_b = [C.cst]
            p.op("dve", lambda en: en.tensor_scalar(out=work[:, 0:L], in0=Iacc[:, 0:L], scalar1=thr, scalar2=None,
                                                    op0=ALU.is_ge), [Iacc] + thr_b, [work])
            for j4 in range(0, nb, 4):
                n4 = min(4, nb - j4)
                for jj in range(n4):
                    p.op("pe", lambda en: en.transpose(acc[:, jj * 128:(jj + 1) * 128],
                                                       work[:, (j4 + jj) * 128:(j4 + jj + 1) * 128], C.ident[:]),
                         [work, C.ident], [acc])
                p.op("act", lambda en: en.copy(maskT[:, j4:j4 + n4, :],
                                               acc[:, 0:n4 * 128].rearrange("p (a b) -> p a b", a=n4)), [acc], [maskT])
            for half in range(2):
                for j in range(nb):
                    for quad in range(2):
                        h0 = half * 8 + quad * 4
                        Oc = C.bank[3 + 2 * quad]
                        Lc = C.bank[4 + 2 * quad]
                        sb_ = C.bank[pti % 3]
                        pt = pts[pti % 3]
                        pti += 1
                        p.op("pe", lambda en: en.matmul(sb_[:], kdr[:, j * 128:(j + 1) * 128], qd[:, h0:h0 + 4, :],
                                                        start=True, stop=True), [kdr, qd], [sb_])
                        p.op("act", lambda en: en.activation(out=pt[:], in_=sb_[:], func=AF.Exp, scale=SA_SCALE),
                             [sb_], [pt])
                        p.op("dve", lambda en: en.tensor_tensor(
                            out=pt[:].rearrange("p (a b) -> p a b", a=4), in0=pt[:].rearrange("p (a b) -> p a b", a=4),
                            in1=maskT[:, j:j + 1, :].to_broadcast([128, 4, 128]), op=ALU.mult), [pt, maskT], [pt])
                        p.op("pe", lambda en: en.matmul(Oc[:], vtok[:, j, :], pt[:], start=(j == 0),
                                                        stop=(j == nb - 1)), [vtok, pt], [Oc])
                        p.op("pe", lambda en: en.matmul(Lc[:], C.ones_bf[:], pt[:], start=(j == 0),
                                                        stop=(j == nb - 1)), [C.ones_bf, pt], [Lc])
                for quad in range(2):
                    h0 = half * 8 + quad * 4
                    Oc = C.bank[3 + 2 * quad]
                    Lc = C.bank[4 + 2 * quad]
                    ob = osb[quad]
                    p.op("dve", lambda en: en.reciprocal(rcp[:], Lc[:]), [Lc], [rcp])
                    p.op("dve", lambda en: en.tensor_tensor(out=ob[:], in0=Oc[:], in1=rcp[:], op=ALU.mult),
                         [Oc, rcp], [ob])
                    p.dma("act", C.mT[2048 + h0 * 128:2048 + (h0 + 4) * 128, q0:q0 + 128].rearrange(
                        "(h d) q -> d h q", d=128), ob[:].rearrange("p (a b) -> p a b", a=4), reads=[ob], writes=[C.mT])


SEG = 512
CDEC = math.exp(-0.5)
NPRM = 181


def rwkv_part(p, C, o, T):
    import os
    RWS = int(os.environ.get("RW_STOP", "99"))
    seg = min(SEG, T)
    nseg = T // seg
    NCH = seg // 64
    NM = NCH * 2
    NG = NM // 4
    W2, A2, V2, G2 = C.lora[o]
    with p.phase():
        rwp = p.sb([128, NPRM], F32, "R_rwp")
        omm = p.sb([128, 53], F32, "R_omm")
        omka = p.sb([128, 16], F32, "R_omka")
        w2b = p.sb([128, 2048], BF16, "R_w2b")
        a2b = p.sb([128, 2048], BF16, "R_a2b")
        v2b = p.sb([128, 2048], BF16, "R_v2b")
        g2b = p.sb([128, 2, 2048], BF16, "R_g2b")
        stg = p.sb([128, 2048], F32, "R_stg")
        cm = p.sb([128, seg], F32, "R_cm")
        m320 = p.sb([128, 512], F32, "R_m320")
        twd = p.sb([128, T], BF16, "R_twd")
        sad = p.sb([128, T], BF16, "R_sad")
        svd = p.sb([128, T], BF16, "R_svd")
        sgd = p.sb([128, 2, T], BF16, "R_sgd")
        xr = p.sb([128, seg + 1], F32, "R_xr")
        names = ["rS", "kS", "vS", "sw", "aG", "t1", "t2", "kk", "Ls", "E1", "E2", "E3", "bt", "kt", "gT", "y", "ym"]
        tl = {n: p.sb([128, seg], F32, "R_" + n) for n in names}
        rS, kS, vS, sw, aG, t1, t2, kk, Ls, E1, E2, E3, bt32, kt32, gT, y, ym = [tl[n] for n in names]
        AR = p.sb([128, NCH, 128], BF16, "R_AR")
        BK = p.sb([128, NCH, 128], BF16, "R_BK")
        BT = p.sb([128, NCH, 128], BF16, "R_BT")
        KT = p.sb([128, NCH, 128], BF16, "R_KT")
        VT = p.sb([128, NCH, 128], BF16, "R_VT")
        VA0 = p.sb([128, NCH, 128], BF16, "R_VA0")
        V0B = p.sb([128, NCH, 128], BF16, "R_V0B")
        Mm = p.sb([128, NCH, 2, 256], BF16, "R_Mm")
        XXg = [p.sb([128, 4, 2, 64], F32, f"R_XX{g}") for g in range(NG)]
        RRg = [p.sb([128, 4, 2, 64], F32, f"R_RR{g}") for g in range(NG)]
        X2g = [p.sb([128, 4, 2, 64], F32, f"R_X2{g}") for g in range(NG)]
        EE = p.sb([128, 384], BF16, "R_EE")
        Eb = p.sb([128, 128], BF16, "R_Eb")
        Ff = p.sb([128, 128], F32, "R_Ff")
        ST = p.sb([128, 128], F32, "R_ST")
        tS = p.sb([128, 128], F32, "R_tS")
        STb = p.sb([128, 128], BF16, "R_STb")
        ob = [p.sb([128, seg], BF16, f"R_ob{i}") for i in range(2)]
        vb = p.sb([128, seg], BF16, "R_vb")
        acc = C.bank[7]

        p.dma("act", rwp[:], C.rwp.t[:, o, :], reads=[C.rwp], writes=[rwp])
        p.dma("act", cm[:], C.cmd.t[:, 0:seg], reads=[C.cmd], writes=[cm])
        p.dma("act", m320[0:64, 0:320], C.m320d.t[:, :], reads=[C.m320d], writes=[m320])
        p.op("dve", lambda en: en.tensor_scalar(out=omm[:], in0=rwp[:, 0:53], scalar1=-1.0, scalar2=1.0,
                                                op0=ALU.mult, op1=ALU.add), [rwp], [omm])
        p.op("dve", lambda en: en.tensor_scalar(out=omka[:], in0=rwp[:, 117:133], scalar1=-1.0, scalar2=1.0,
                                                op0=ALU.mult, op1=ALU.add), [rwp], [omka])
        for (wsrc, nr, dstb) in ((W2, 96, w2b), (A2, 96, a2b), (V2, 64, v2b)):
            p.dma("act", stg[0:nr, :], wsrc.t[:, :], reads=[wsrc], writes=[stg])
            p.op("pool", lambda en: en.tensor_copy(dstb[0:nr, :], stg[0:nr, :]), [stg], [dstb])
        for c2 in range(2):
            p.dma("act", stg[:, :], G2.t[c2 * 128:(c2 + 1) * 128, :], reads=[G2], writes=[stg])
            p.op("pool", lambda en: en.tensor_copy(g2b[:, c2, :], stg[:, :]), [stg], [g2b])
        for t_ in (VA0, V0B, EE):
            p.op("dve", lambda en: en.memset(t_[:], 0.0), [], [t_])

        def load_shift(dst, r0, nr, mucol, sg):
            s0 = sg * seg
            if sg == 0:
                p.op("dve", lambda en: en.memset(xr[0:nr, 0:1], 0.0), [], [xr])
                p.dma("act", xr[0:nr, 1:seg + 1], C.zT[r0:r0 + nr, 0:seg], reads=[C.zT], writes=[xr])
            else:
                p.dma("act", xr[0:nr, 0:seg + 1], C.zT[r0:r0 + nr, s0 - 1:s0 + seg], reads=[C.zT], writes=[xr])
            p.op("dve", lambda en: en.tensor_scalar(out=dst[0:nr, :], in0=xr[0:nr, 1:seg + 1],
                                                    scalar1=omm[0:nr, mucol:mucol + 1], scalar2=None, op0=ALU.mult),
                 [xr, omm], [dst])
            p.op("dve", lambda en: en.scalar_tensor_tensor(out=dst[0:nr, :], in0=xr[0:nr, 0:seg],
                                                           scalar=rwp[0:nr, mucol:mucol + 1], in1=dst[0:nr, :],
                                                           op0=ALU.mult, op1=ALU.add), [xr, rwp, dst], [dst])

        for sg in range(nseg):
            ss_ = slice(sg * seg, (sg + 1) * seg)
            load_shift(t1, 6144, 96, 48, sg)
            p.op("act", lambda en: en.activation(out=twd[0:96, ss_], in_=t1[0:96, :], func=AF.Tanh), [t1], [twd])
            load_shift(t1, 6240, 96, 49, sg)
            p.op("act", lambda en: en.copy(sad[0:96, ss_], t1[0:96, :]), [t1], [sad])
            load_shift(t1, 6336, 64, 50, sg)
            p.op("act", lambda en: en.copy(svd[0:64, ss_], t1[0:64, :]), [t1], [svd])
            for c2 in range(2):
                load_shift(t1, 6400 + c2 * 128, 128, 51 + c2, sg)
                p.op("act", lambda en: en.activation(out=sgd[:, c2, ss_], in_=t1[:, :], func=AF.Sigmoid), [t1], [sgd])

        col = lambda base, c: rwp[:, base + c:base + c + 1]
        bi = 0
        if RWS == 1:
            return
        for c in range(16):
            cs = slice(c * 128, (c + 1) * 128)
            p.op("dve", lambda en: en.memset(ST[:], 0.0), [], [ST])
            p.op("dve", lambda en: en.memset(STb[:], 0.0), [], [STb])
            for sg in range(nseg):
                s0 = sg * seg
                ss_ = slice(s0, s0 + seg)
                load_shift(rS, c * 128, 128, c, sg)
                load_shift(kS, 2048 + c * 128, 128, 16 + c, sg)
                load_shift(vS, 4096 + c * 128, 128, 32 + c, sg)
                p.dma("act", t2[:], C.vfirst[cs, ss_], reads=[C.vfirst], writes=[t2])
                p.op("pe", lambda en: en.matmul(acc[:, 0:seg], w2b[0:96, cs], twd[0:96, ss_], start=True, stop=True),
                     [w2b, twd], [acc])
                p.op("act", lambda en: en.activation(out=sw[:], in_=acc[:, 0:seg], func=AF.Sigmoid,
                                                     bias=col(53, c)), [acc, rwp], [sw])
                p.op("pe", lambda en: en.matmul(acc[:, 0:seg], a2b[0:96, cs], sad[0:96, ss_], start=True, stop=True),
                     [a2b, sad], [acc])
                p.op("act", lambda en: en.activation(out=aG[:], in_=acc[:, 0:seg], func=AF.Sigmoid,
                                                     bias=col(69, c)), [acc, rwp], [aG])
                p.op("pe", lambda en: en.matmul(acc[:, 0:seg], v2b[0:64, cs], svd[0:64, ss_], start=True, stop=True),
                     [v2b, svd], [acc])
                p.op("act", lambda en: en.activation(out=t1[:], in_=acc[:, 0:seg], func=AF.Sigmoid,
                                                     bias=col(85, c)), [acc, rwp], [t1])
                p.op("dve", lambda en: en.tensor_tensor(out=t2[:], in0=t2[:], in1=vS[:], op=ALU.subtract),
                     [t2, vS], [t2])
                p.op("dve", lambda en: en.tensor_tensor(out=t2[:], in0=t2[:], in1=t1[:], op=ALU.mult), [t2, t1], [t2])
                p.op("dve", lambda en: en.tensor_tensor(out=vS[:], in0=vS[:], in1=t2[:], op=ALU.add), [vS, t2], [vS])
                for c2 in range(2):
                    p.op("pe", lambda en: en.matmul(acc[:, 0:seg], g2b[:, c2, cs], sgd[:, c2, ss_], start=(c2 == 0),
                                                    stop=(c2 == 1)), [g2b, sgd], [acc])
                p.op("act", lambda en: en.copy(gT[:], acc[:, 0:seg]), [acc], [gT])
                p.op("dve", lambda en: en.tensor_scalar(out=kk[:], in0=kS[:], scalar1=col(101, c), scalar2=None,
                                                        op0=ALU.mult), [kS, rwp], [kk])
                p.op("act", lambda en: en.activation(out=t1[:], in_=kk[:], func=AF.Square), [kk], [t1])
                p.op("pe", lambda en: en.matmul(acc[:, 0:seg], C.bd64[:], t1[:], start=True, stop=True),
                     [C.bd64, t1], [acc])
                p.op("act", lambda en: en.activation(out=t2[:], in_=acc[:, 0:seg], func=AF.Sqrt, bias=C.cst[:, 3:4]),
                     [acc, C.cst], [t2])
                p.op("dve", lambda en: en.reciprocal(t2[:], t2[:]), [t2], [t2])
                p.op("dve", lambda en: en.tensor_tensor(out=kk[:], in0=kk[:], in1=t2[:], op=ALU.mult), [kk, t2], [kk])
                p.op("dve", lambda en: en.tensor_scalar(out=t1[:], in0=aG[:], scalar1=col(117, c),
                                                        scalar2=omka[:, c:c + 1], op0=ALU.mult, op1=ALU.add),
                     [aG, rwp, omka], [t1])
                p.op("dve", lambda en: en.tensor_tensor(out=kS[:], in0=kS[:], in1=t1[:], op=ALU.mult), [kS, t1], [kS])
                p.op("dve", lambda en: en.tensor_tensor_scan(out=Ls[:], data0=cm[:], data1=sw[:], initial=0.0,
                                                             op0=ALU.mult, op1=ALU.add), [cm, sw], [Ls])
                p.op("act", lambda en: en.activation(out=E1[:], in_=Ls[:], func=AF.Exp, scale=-CDEC), [Ls], [E1])
                p.op("act", lambda en: en.activation(out=E2[:], in_=Ls[:], func=AF.Exp, scale=CDEC), [Ls], [E2])
                p.op("dve", lambda en: en.tensor_tensor(out=t1[:], in0=Ls[:], in1=sw[:], op=ALU.subtract),
                     [Ls, sw], [t1])
                p.op("act", lambda en: en.activation(out=E3[:], in_=t1[:], func=AF.Exp, scale=-CDEC), [t1], [E3])
                v3 = lambda b_: b_[:].rearrange("p (a b) -> p a b", b=64)
                p.op("dve", lambda en: en.scalar_tensor_tensor(out=AR[:, :, 0:64], in0=v3(kk), scalar=-1.0, in1=v3(E3),
                                                               op0=ALU.mult, op1=ALU.mult), [kk, E3], [AR])
                p.op("dve", lambda en: en.tensor_tensor(out=AR[:, :, 64:128], in0=v3(rS), in1=v3(E1), op=ALU.mult),
                     [rS, E1], [AR])
                p.op("dve", lambda en: en.tensor_tensor(out=t1[:], in0=kk[:], in1=aG[:], op=ALU.mult), [kk, aG], [t1])
                p.op("dve", lambda en: en.tensor_tensor(out=bt32[:], in0=t1[:], in1=E2[:], op=ALU.mult),
                     [t1, E2], [bt32])
                p.op("dve", lambda en: en.tensor_tensor(out=kt32[:], in0=kS[:], in1=E2[:], op=ALU.mult),
                     [kS, E2], [kt32])
                p.op("act", lambda en: en.copy(BK[:, :, 0:64], v3(bt32)), [bt32], [BK])
                p.op("act", lambda en: en.copy(BK[:, :, 64:128], v3(kt32)), [kt32], [BK])
                if RWS == 2:
                    return
                p.op("act", lambda en: en.copy(vb[:], vS[:]), [vS], [vb])
                for (srcf, dsts) in ((lambda ch: BK[:, ch, 0:64], (BT,)), (lambda ch: BK[:, ch, 64:128], (KT,)),
                                     (lambda ch: vb[:, ch * 64:(ch + 1) * 64], (VT, VA0, V0B))):
                    for c4 in range(0, NCH, 4):
                        bk = C.bank[bi % 3]
                        bi += 1
                        for jj in range(4):
                            ch = c4 + jj
                            p.op("pe", lambda en: en.matmul(bk[0:64, jj * 128:(jj + 1) * 128], srcf(ch),
                                                            C.ident_bf[:], start=True, stop=True),
                                 [BK, vb, C.ident_bf], [bk])
                        bv = bk[0:64, :].rearrange("p (a b) -> p a b", a=4)
                        p.op("dve", lambda en: en.tensor_copy(dsts[0][0:64, c4:c4 + 4, :], bv), [bk], [dsts[0]])
                        if len(dsts) == 3:
                            p.op("dve", lambda en: en.tensor_copy(dsts[1][0:64, c4:c4 + 4, 0:64], bv[:, :, 0:64]),
                                 [bk], [dsts[1]])
                            p.op("dve", lambda en: en.tensor_copy(dsts[2][0:64, c4:c4 + 4, 64:128], bv[:, :, 64:128]),
                                 [bk], [dsts[2]])
                if RWS == 3:
                    return
                for ch in range(NCH):
                    for hh in range(2):
                        hs = slice(hh * 64, hh * 64 + 64)
                        m = ch * 2 + hh
                        bk = C.bank[bi % 3]
                        bi += 1
                        p.op("pe", lambda en: en.matmul(bk[0:64, 0:128], BK[hs, ch, 0:64], AR[hs, ch, :], start=True,
                                                        stop=True), [BK, AR], [bk])
                        p.op("pe", lambda en: en.matmul(bk[0:64, 128:256], BK[hs, ch, 64:128], AR[hs, ch, :],
                                                        start=True, stop=True), [BK, AR], [bk])
                        p.op("pe", lambda en: en.matmul(bk[0:64, 256:320], AR[hs, ch, 0:64], BK[hs, ch, 0:64],
                                                        start=True, stop=True), [BK, AR], [bk])
                        p.op("dve", lambda en: en.tensor_tensor(out=Mm[0:64, ch, hh, :], in0=bk[0:64, 0:256],
                                                                in1=m320[0:64, 0:256], op=ALU.mult), [bk, m320], [Mm])
                        xg = XXg[m // 4]
                        p.op("dve", lambda en: en.tensor_tensor(
                            out=xg[0:64, m % 4, :, :],
                            in0=bk[0:64, :].rearrange("p (a b) -> p a b", b=256)[:, :, 0:64],
                            in1=m320[0:64, :].rearrange("p (a b) -> p a b", b=256)[:, :, 0:64], op=ALU.mult),
                            [bk, m320], [xg])
                if RWS == 4:
                    return
                Xc, Xn = XXg, X2g
                for g in range(NG):
                    p.op("dve", lambda en: en.tensor_tensor(
                        out=RRg[g][0:64].rearrange("p a b c -> p (a b) c"),
                        in0=Xc[g][0:64].rearrange("p a b c -> p (a b) c"),
                        in1=C.ident[0:64, None, 0:64].to_broadcast([64, 8, 64]), op=ALU.add), [Xc[g], C.ident], [RRg[g]])
                for st_ in range(5):
                    last = (st_ == 4)
                    for g in range(NG):
                        bk = C.bank[bi % 3]
                        bi += 1
                        for mm in range(4):
                            p.op("pe", lambda en: en.matmul(bk[0:64, mm * 128:mm * 128 + 64], Xc[g][0:64, mm, 1, :],
                                                            Xc[g][0:64, mm, 0, :], start=True, stop=True), [Xc[g]], [bk])
                            if not last:
                                p.op("pe", lambda en: en.matmul(bk[0:64, mm * 128 + 64:mm * 128 + 128],
                                                                Xc[g][0:64, mm, 0, :], Xc[g][0:64, mm, 1, :],
                                                                start=True, stop=True), [Xc[g]], [bk])
                        if not last:
                            p.op("act", lambda en: en.copy(Xn[g][0:64].rearrange("p a b c -> p (a b c)"), bk[0:64, :]),
                                 [bk], [Xn[g]])
                        else:
                            p.op("act", lambda en: en.copy(
                                Xn[g][0:64, :, 0, :], bk[0:64, :].rearrange("p (a b) -> p a b", b=128)[:, :, 0:64]),
                                [bk], [Xn[g]])
                    for g in range(NG):
                        bk = C.bank[bi % 3]
                        bi += 1
                        for mm in range(4):
                            p.op("pe", lambda en: en.matmul(bk[0:64, mm * 128:mm * 128 + 64], RRg[g][0:64, mm, 1, :],
                                                            Xn[g][0:64, mm, 0, :], start=True, stop=True),
                                 [RRg[g], Xn[g]], [bk])
                            if not last:
                                p.op("pe", lambda en: en.matmul(bk[0:64, mm * 128 + 64:mm * 128 + 128],
                                                                Xn[g][0:64, mm, 0, :], RRg[g][0:64, mm, 1, :],
                                                                start=True, stop=True), [RRg[g], Xn[g]], [bk])
                        if not last:
                            p.op("dve", lambda en: en.tensor_tensor(
                                out=RRg[g][0:64].rearrange("p a b c -> p (a b c)"),
                                in0=RRg[g][0:64].rearrange("p a b c -> p (a b c)"), in1=bk[0:64, :], op=ALU.add),
                                [RRg[g], bk], [RRg[g]])
                        else:
                            p.op("dve", lambda en: en.tensor_tensor(
                                out=RRg[g][0:64, :, 0, :], in0=RRg[g][0:64, :, 0, :],
                                in1=bk[0:64, :].rearrange("p (a b) -> p a b", b=128)[:, :, 0:64], op=ALU.add),
                                [RRg[g], bk], [RRg[g]])
                    Xc, Xn = Xn, Xc
                if RWS == 5:
                    return
                bF, bE, bU, bY = C.bank[3], C.bank[4], C.bank[5], C.bank[6]
                for ch in range(NCH):
                    mA, mB = ch * 2, ch * 2 + 1
                    p.op("pe", lambda en: en.matmul(bF[0:64, 0:128], AR[:, ch, 0:64], STb[:], start=True, stop=False),
                         [AR, STb], [bF])
                    p.op("pe", lambda en: en.matmul(bF[0:64, 0:128], Mm[0:64, ch, 0, 128:192], VA0[0:64, ch, :],
                                                    start=False, stop=False), [Mm, VA0], [bF])
                    p.op("pe", lambda en: en.matmul(bF[0:64, 0:128], Mm[0:64, ch, 1, 128:192], V0B[0:64, ch, :],
                                                    start=False, stop=True), [Mm, V0B], [bF])
                    p.op("act", lambda en: en.copy(Ff[0:64, :], bF[0:64, 0:128]), [bF], [Ff])
                    for hh, m in ((0, mA), (1, mB)):
                        p.op("pe", lambda en: en.matmul(bE[0:64, hh * 64:hh * 64 + 64], RRg[m // 4][0:64, m % 4, 0, :],
                                                        Ff[0:64, hh * 64:hh * 64 + 64], start=True, stop=True),
                             [RRg[m // 4], Ff], [bE])
                    p.op("dve", lambda en: en.tensor_copy(Eb[0:64, :], bE[0:64, 0:128]), [bE], [Eb])
                    p.op("dve", lambda en: en.tensor_copy(
                        EE[0:64, :].rearrange("p (a b) -> p a b", b=192)[:, :, 0:64],
                        bE[0:64, 0:128].rearrange("p (a b) -> p a b", b=64)), [bE], [EE])
                    ys = slice(ch * 64, ch * 64 + 64)
                    p.op("pe", lambda en: en.matmul(bY[:, ys], STb[:], AR[:, ch, 64:128], start=True, stop=False),
                         [STb, AR], [bY])
                    p.op("pe", lambda en: en.matmul(bY[:, ys], EE[0:64, 0:128], Mm[0:64, ch, 0, 64:128], start=False,
                                                    stop=False), [EE, Mm], [bY])
                    p.op("pe", lambda en: en.matmul(bY[:, ys], EE[0:64, 128:256], Mm[0:64, ch, 1, 64:128], start=False,
                                                    stop=False), [EE, Mm], [bY])
                    p.op("pe", lambda en: en.matmul(bY[:, ys], VA0[0:64, ch, :], Mm[0:64, ch, 0, 192:256], start=False,
                                                    stop=False), [VA0, Mm], [bY])
                    p.op("pe", lambda en: en.matmul(bY[:, ys], V0B[0:64, ch, :], Mm[0:64, ch, 1, 192:256], start=False,
                                                    stop=True), [V0B, Mm], [bY])
                    p.op("pe", lambda en: en.matmul(bU[:, 0:128], BT[0:64, ch, :], Eb[0:64, :], start=True, stop=False),
                         [BT, Eb], [bU])
                    p.op("pe", lambda en: en.matmul(bU[:, 0:128], KT[0:64, ch, :], VT[0:64, ch, :], start=False,
                                                    stop=True), [KT, VT], [bU])
                    p.op("dve", lambda en: en.tensor_tensor(out=tS[:], in0=bU[:, 0:128], in1=ST[:], op=ALU.add),
                         [bU, ST], [tS])
                    pc = E1[:, ch * 64 + 63:ch * 64 + 64]
                    p.op("dve", lambda en: en.scalar_tensor_tensor(out=ST[:], in0=tS[:], scalar=pc, in1=C.bd64[:],
                                                                   op0=ALU.mult, op1=ALU.mult), [tS, E1, C.bd64], [ST])
                    p.op("act", lambda en: en.copy(STb[:], ST[:]), [ST], [STb])
                p.op("act", lambda en: en.copy(y[:], bY[:, 0:seg]), [bY], [y])
                if RWS == 6:
                    return
                p.op("pe", lambda en: en.matmul(acc[:, 0:seg], C.bd64[:], y[:], start=True, stop=True),
                     [C.bd64, y], [acc])
                p.op("dve", lambda en: en.scalar_tensor_tensor(out=ym[:], in0=acc[:, 0:seg], scalar=-1.0 / 64, in1=y[:],
                                                               op0=ALU.mult, op1=ALU.add), [acc, y], [ym])
                p.op("act", lambda en: en.activation(out=t1[:], in_=ym[:], func=AF.Square), [ym], [t1])
                p.op("pe", lambda en: en.matmul(acc[:, 0:seg], C.bd64[:], t1[:], start=True, stop=True),
                     [C.bd64, t1], [acc])
                p.op("act", lambda en: en.activation(out=t2[:], in_=acc[:, 0:seg], func=AF.Sqrt, scale=1.0 / 64,
                                                     bias=C.cst[:, 2:3]), [acc, C.cst], [t2])
                p.op("dve", lambda en: en.reciprocal(t2[:], t2[:]), [t2], [t2])
                p.op("dve", lambda en: en.scalar_tensor_tensor(out=ym[:], in0=ym[:], scalar=col(149, c), in1=t2[:],
                                                               op0=ALU.mult, op1=ALU.mult), [ym, t2, rwp], [ym])
                p.op("dve", lambda en: en.tensor_scalar(out=ym[:], in0=ym[:], scalar1=col(165, c), scalar2=None,
                                                        op0=ALU.add), [ym, rwp], [ym])
                p.op("dve", lambda en: en.scalar_tensor_tensor(out=t1[:], in0=rS[:], scalar=col(133, c), in1=kS[:],
                                                               op0=ALU.mult, op1=ALU.mult), [rS, kS, rwp], [t1])
                p.op("pe", lambda en: en.matmul(acc[:, 0:seg], C.bd64[:], t1[:], start=True, stop=True),
                     [C.bd64, t1], [acc])
                p.op("dve", lambda en: en.tensor_tensor(out=t2[:], in0=acc[:, 0:seg], in1=vS[:], op=ALU.mult),
                     [acc, vS], [t2])
                p.op("dve", lambda en: en.tensor_tensor(out=ym[:], in0=ym[:], in1=t2[:], op=ALU.add), [ym, t2], [ym])
                obb = ob[sg % 2]
                p.op("dve", lambda en: en.tensor_tensor(out=obb[:], in0=ym[:], in1=gT[:], op=ALU.mult), [ym, gT], [obb])
                p.dma("act", C.mT[cs, ss_], obb[:], reads=[obb], writes=[C.mT])


def mixer_odd(p, C, o, T, parts="rd"):
    if "r" in parts:
        rwkv_part(p, C, o, T)
    if "d" in parts:
        dsa_prep(p, C, o, T)
        dsa_attn(p, C, o, T)


def build(T, layers, stop_after=None, debug=False, vfirst_in=False, parts="rd"):
    p = Prog()
    C = Ctx()
    C.gpar = 0
    C.wi = 0
    C.used = []
    xin = p.dram("xT", [D, T], F32, "ExternalInput")
    out = p.dram("outT", [D, T], F32, "ExternalOutput")

    def wd(name, shape):
        C.used.append(name)
        return p.dram(name, shape, F32, "ExternalInput")

    gn_d = p.dram("gn", [128, 8, KC], F32, "ExternalInput")
    sm_d = p.dram("sm", [128, 8], F32, "ExternalInput")
    cw_d = p.dram("cw", [128, 96], F32, "ExternalInput")
    C.dalam = p.dram("dalam", [2, 256], F32, "ExternalInput")
    cf_d = p.dram("cf32", [128, 5 * 128 + 4], F32, "ExternalInput")
    has_odd = any(l % 2 == 1 for l in layers)
    NB = T // 128
    if has_odd:
        sm2_d = p.dram("sm2", [128, 6], F32, "ExternalInput")
        C.rope128c = p.dram("rope128c", [128, T], F32, "ExternalInput")
        C.rope128s = p.dram("rope128s", [128, T], F32, "ExternalInput")
        C.trid = p.dram("trid", [128, 128], F32, "ExternalInput")
        C.rwp = p.dram("rwp", [128, 2, NPRM], F32, "ExternalInput")
        C.cmd = p.dram("cmd", [128, SEG], F32, "ExternalInput")
        C.m320d = p.dram("m320d", [64, 320], F32, "ExternalInput")
        C.lora = {}
        C.kdr = p.dram("kdr", [128, T], BF16, "Internal")
        C.kir = p.dram("kir", [64, T], BF16, "Internal")
        C.vtk = p.dram("vtk", [128, NB, 128], BF16, "Internal")
        C.wtk = p.dram("wtk", [128, NB, 16], F32, "Internal")
        C.qdr = p.dram("qdr", [2048, T], BF16, "Internal")
        C.qir = p.dram("qir", [1024, T], BF16, "Internal")
    C.rope64c = p.dram("rope64c", [128, T], F32, "ExternalInput")
    C.rope64s = p.dram("rope64s", [128, T], F32, "ExternalInput")
    C.dmask = p.dram("dmask", [128, 4, 512], BF16, "ExternalInput")
    dk = "ExternalOutput" if debug else "Internal"
    C.zT = p.dram("zT", [EV_IN, T], F32, dk)
    C.mT = p.dram("mT", [D, T], BF16, dk)
    C.vfirst = p.dram("vfirst", [2048, T], F32, "ExternalInput" if vfirst_in else "Internal")

    C.bank = [p.ps(name=f"bank{i}") for i in range(8)]
    C.ci = 0
    C.Wb_in = p.dram("Wb_in", [24, 128, KC, 512], BF16, "Internal")
    C.Wb_out = p.dram("Wb_out", [8, 128, KC, 512], BF16, "Internal")
    C.Wb_gu = p.dram("Wb_gu", [43, 128, KC, 512], BF16, "Internal")
    C.Wb_dn = p.dram("Wb_dn", [8, 128, FC, 512], BF16, "Internal")
    C.sqb = [p.sb([128, TT], BF16, f"sqb{i}") for i in range(2)]
    C.rstd = p.sb([128, TT], F32, "rstd")
    C.gn = p.sb([128, 8, KC], F32, "gn")
    C.sm = p.sb([128, 8], F32, "sm")
    C.cw = p.sb([128, 96], F32, "cw")
    C.ones_f = p.sb([128, 128], F32, "ones_f")
    C.bd64 = p.sb([128, 128], F32, "bd64")
    C.ident = p.sb([128, 128], F32, "ident")
    C.rm64 = p.sb([128, 128], F32, "rm64")
    C.rm128 = p.sb([128, 128], F32, "rm128")
    C.cst = p.sb([128, 4], F32, "cst")
    C.ones_bf = p.sb([128, 128], BF16, "ones_bf")
    if has_odd:
        C.sm2 = p.sb([128, 6], F32, "sm2")
        p.dma("sp", C.sm2[:], sm2_d[:], reads=[sm2_d], writes=[C.sm2])
    p.dma("sp", C.gn[:], gn_d[:], reads=[gn_d], writes=[C.gn])
    p.dma("sp", C.sm[:], sm_d[:], reads=[sm_d], writes=[C.sm])
    p.dma("sp", C.cw[:], cw_d[:], reads=[cw_d], writes=[C.cw])
    for i, b in enumerate((C.ones_f, C.bd64, C.ident, C.rm64, C.rm128)):
        p.dma("sp", b[:], cf_d[:, i * 128:(i + 1) * 128], reads=[cf_d], writes=[b])
    p.dma("sp", C.cst[:], cf_d[:, 640:644], reads=[cf_d], writes=[C.cst])
    p.op("dve", lambda e: e.tensor_copy(C.ones_bf[:], C.ones_f[:]), [C.ones_f], [C.ones_bf])
    C.ident_bf = p.sb([128, 128], BF16, "ident_bf")
    p.op("dve", lambda e: e.tensor_copy(C.ident_bf[:], C.ident[:]), [C.ident], [C.ident_bf])

    for li, l in enumerate(layers):
        src = xin if li == 0 else out
        nin = EV_IN if l % 2 == 0 else OD_IN
        phase_in(p, C, l, src, wd(f"win{l}", [D, nin]), nin, T)
        if stop_after == ("in", l):
            break
        if l % 2 == 0:
            mixer_even(p, C, l // 2, T, save_vfirst=(l == 0))
        else:
            o = l // 2
            C.lora[o] = (wd(f"w2_{o}", [96, 2048]), wd(f"a2_{o}", [96, 2048]), wd(f"v2_{o}", [64, 2048]),
                         wd(f"g2_{o}", [256, 2048]))
            mixer_odd(p, C, o, T, parts)
        if stop_after == ("mix", l):
            break
        phase_out_ffn(p, C, l, src, out, wd(f"wout{l}", [D, D]), wd(f"gate{l}", [D, FH]), wd(f"up{l}", [D, FH]),
                      wd(f"down{l}", [FH, D]), T)
    p.barrier()
    p.es.close()
    return p, C


def rope_tables(T, rows, half):
    inv = (10000.0 ** (-np.arange(half, dtype=np.float32) / half)).astype(np.float32)
    ang = np.arange(T, dtype=np.float32)[None, :] * inv[:, None]
    idx = np.arange(rows) % half
    return np.cos(ang)[idx].astype(np.float32), np.sin(ang)[idx].astype(np.float32)


def rot_matrix(gsz):
    m = np.zeros((128, 128), np.float32)
    half = gsz // 2
    for g0 in range(0, 128, gsz):
        for d in range(half):
            m[g0 + d + half, g0 + d] = -1.0
            m[g0 + d, g0 + d + half] = 1.0
    return m


def host_consts(T):
    cf = np.zeros((128, 5 * 128 + 4), np.float32)
    cf[:, 0:128] = 1.0
    bd = np.zeros((128, 128), np.float32)
    bd[0:64, 0:64] = 1.0
    bd[64:128, 64:128] = 1.0
    cf[:, 128:256] = bd
    cf[:, 256:384] = np.eye(128, dtype=np.float32)
    cf[:, 384:512] = rot_matrix(64)
    cf[:, 512:640] = rot_matrix(128)
    cf[:, 640] = EPS
    cf[:, 641] = -1.0e29
    cf[:, 642] = 64e-5
    cf[:, 643] = 1.0e-24
    c64, s64 = rope_tables(T, 128, 32)
    c128, s128 = rope_tables(T, 128, 64)
    ii = np.arange(128)
    trid = np.where(ii[None, :] <= ii[:, None], 0.0, NEG).astype(np.float32)
    cmd = np.ones((128, SEG), np.float32)
    cmd[:, ::64] = 0.0
    i64 = np.arange(64)
    su = (i64[:, None] < i64[None, :]).astype(np.float32)
    iu = (i64[:, None] <= i64[None, :]).astype(np.float32)
    sl = (i64[:, None] > i64[None, :]).astype(np.float32)
    m320 = np.concatenate([su, iu, su, iu, sl], axis=1)
    s_ = np.arange(128)[:, None, None]
    jj = np.arange(4)[None, :, None]
    q_ = np.arange(512)[None, None, :]
    dmask = ((jj * 128 + s_) <= q_).astype(np.float32).astype(ml_dtypes.bfloat16)
    return dict(cf32=cf, rope64c=c64, rope64s=s64, dmask=dmask, rope128c=c128, rope128s=s128, trid=trid, cmd=cmd,
                m320d=m320)


def host_params(inp):
    gn = np.zeros((128, 8, KC), np.float32)
    for l in range(4):
        gn[:, l, :] = inp["mix_norm"][l].reshape(KC, 128).T
        gn[:, 4 + l, :] = inp["ffn_norm"][l].reshape(KC, 128).T
    sm = np.zeros((128, 8), np.float32)
    for e in range(2):
        sm[:, e] = np.tile(inp["da_q_norm"][e], 2)
        sm[:, 2 + e] = np.tile(inp["da_k_norm"][e], 2)
        sm[:, 4 + e] = inp["da_subln"][e]
    cw = np.zeros((128, 2, 3, 16), np.float32)
    for e in range(2):
        for j in range(3):
            cw[:, e, j, :] = inp["sc_conv"][e, j].reshape(16, 128).T
    out = dict(gn=gn, sm=sm, cw=cw.reshape(128, 96),
               dalam=np.ascontiguousarray(inp["da_lambda"].reshape(2, 256)))
    if "rw_mu" in inp:
        sm2 = np.zeros((128, 6), np.float32)
        rwp = np.zeros((128, 2, NPRM), np.float32)
        for o in range(2):
            sm2[:, o] = inp["sa_q_norm"][o]
            sm2[:, 2 + o] = inp["sa_k_norm"][o]
            sm2[:, 4 + o] = np.tile(inp["idx_k_norm"][o], 2)
            mu = inp["rw_mu"][o]
            rwp[:, o, 0:48] = mu[0:6144].reshape(48, 128).T
            rwp[0:96, o, 48] = mu[6144:6240]
            rwp[0:96, o, 49] = mu[6240:6336]
            rwp[0:64, o, 50] = mu[6336:6400]
            rwp[:, o, 51:53] = mu[6400:6656].reshape(2, 128).T
            for base, key in ((53, "rw_w0"), (69, "rw_a0"), (85, "rw_v0"), (101, "rw_k_k"), (117, "rw_k_a"),
                              (133, "rw_r_k"), (149, "rw_lnx_w"), (165, "rw_lnx_b")):
                rwp[:, o, base:base + 16] = inp[key][o].reshape(16, 128).T
        out["sm2"] = sm2
        out["rwp"] = rwp
    return out


def weight_of(inp, name):
    kind, l = name.rstrip("0123456789"), int(name[-1])
    if kind == "gate":
        return inp["ffn_gate"][l]
    if kind == "up":
        return inp["ffn_up"][l]
    if kind == "down":
        return inp["ffn_down"][l]
    if kind == "win":
        return inp["ev_w_in"][l // 2] if l % 2 == 0 else inp["od_w_in"][l // 2]
    if kind == "wout":
        return inp["ev_w_out"][l // 2] if l % 2 == 0 else inp["od_w_out"][l // 2]
    if kind == "w2_":
        return inp["rw_w2"][l]
    if kind == "a2_":
        return inp["rw_a2"][l]
    if kind == "v2_":
        return inp["rw_v2"][l]
    if kind == "g2_":
        return inp["rw_g2"][l]
    raise KeyError(name)


def make_in_maps(inp, T, used, ncores, names=None):
    cs = host_consts(T)
    ps = host_params(inp)
    maps = []
    for b in range(ncores):
        m = dict(cs)
        m.update(ps)
        m = {k: v for k, v in m.items() if k in names}
        m["xT"] = np.ascontiguousarray(inp["x"][b, :T].T)
        for name in used:
            m[name] = weight_of(inp, name)
        maps.append(m)
    return maps


def kernel(**inp):
    inp = {k: np.asarray(v) for k, v in inp.items()}
    T = 4096
    layers = [0, 1, 2, 3]
    p, C = build(T, layers)
    maps = make_in_maps(inp, T, C.used, 2, set(p.in_names))
    res = run_bass_kernel_spmd(p.nc, maps, core_ids=[0, 1])
    outp = np.stack([res.results[b]["outT"].T for b in range(2)], axis=0)
    return np.ascontiguousarray(outp.astype(np.float32))
```
